# Optimizing a Trainium2 kernel written in Bass

```python
import math
import jax, jax.numpy as jnp
from jax import lax
import numpy as np

D_MODEL = 2048
BATCH = 4
SEQ = 8192
DEPTH = 1

CONV_WIDTH = D_MODEL // 2
CONV_KERNEL = 31
POOL_WIDTH = D_MODEL // 2
POOL_WINDOWS = (2, 4, 8, 16)
N_POOL_GROUPS = len(POOL_WINDOWS)
POOL_GROUP = POOL_WIDTH // N_POOL_GROUPS
POOL_OUT_GROUP = D_MODEL // N_POOL_GROUPS
MAX_WINDOW = max(POOL_WINDOWS)
N_BRANCHES = 2
IN_COLS = 2 * CONV_WIDTH + POOL_WIDTH + N_BRANCHES * D_MODEL
FFN_HIDDEN = int(math.ceil(8 * D_MODEL / 3 / 256)) * 256
DEEPNORM_ALPHA = (2.0 * DEPTH) ** 0.25
DEEPNORM_BETA = (8.0 * DEPTH) ** -0.25
LN_EPS = 1e-5

kernel_name = "hybrid_conv_pool_gated_deepnorm"


def layer_norm(x, g, b):
    xf = x.astype(jnp.float32)
    mu = jnp.mean(xf, axis=-1, keepdims=True)
    xc = xf - mu
    var = jnp.mean(xc * xc, axis=-1, keepdims=True)
    y = xc * lax.rsqrt(var + LN_EPS) * g.astype(jnp.float32) + b.astype(jnp.float32)
    return y.astype(x.dtype)


def conformer_conv_branch(u_glu, dw_kernel, dw_bias, ln_g, ln_b, w_conv_out):
    a, gate = jnp.split(u_glu, 2, axis=-1)
    u = a * jax.nn.sigmoid(gate)
    u = lax.conv_general_dilated(
        u, dw_kernel[:, None, :].astype(u.dtype),
        window_strides=(1,), padding=[(CONV_KERNEL - 1, 0)],
        dimension_numbers=("NWC", "WIO", "NWC"),
        feature_group_count=CONV_WIDTH) + dw_bias
    u = layer_norm(u, ln_g, ln_b)
    u = jax.nn.silu(u)
    return jnp.einsum("bsc,cd->bsd", u, w_conv_out)


def multiscale_pool_branch(u, w_pool, pool_scale):
    B, S, _ = u.shape
    ug = u.reshape(B, S, N_POOL_GROUPS, POOL_GROUP).astype(jnp.float32)
    cs = jnp.cumsum(ug, axis=1)
    cs_pad = jnp.concatenate(
        [jnp.zeros((B, MAX_WINDOW, N_POOL_GROUPS, POOL_GROUP), jnp.float32), cs], axis=1)
    pos = jnp.arange(S, dtype=jnp.float32)[None, :, None]
    pooled = []
    for i, w in enumerate(POOL_WINDOWS):
        lagged = lax.dynamic_slice_in_dim(cs_pad[:, :, i], MAX_WINDOW - w, S, axis=1)
        count = jnp.minimum(pos + 1.0, float(w))
        mean = (cs[:, :, i] - lagged) / count
        pooled.append(mean - ug[:, :, i])
    pooled = jnp.stack(pooled, axis=2).astype(u.dtype)
    y = jnp.einsum("bsgc,gcd->bsgd", pooled, w_pool)
    return y.reshape(B, S, D_MODEL) * pool_scale


def setup_inputs(seed: int = 0) -> dict:
    key = jax.random.key(seed)
    ks = jax.random.split(key, 17)
    L, D = DEPTH, D_MODEL
    f32 = jnp.float32
    nrm = lambda k, shape, s: jax.random.normal(k, shape, f32) * s
    return {
        "x": jax.random.normal(ks[0], (BATCH, SEQ, D), f32),
        "w_in": nrm(ks[1], (L, D, IN_COLS), D ** -0.5),
        "dw_kernel": nrm(ks[2], (L, CONV_KERNEL, CONV_WIDTH), CONV_KERNEL ** -0.5),
        "dw_bias": nrm(ks[3], (L, CONV_WIDTH), 0.02),
        "conv_ln_g": 1.0 + nrm(ks[4], (L, CONV_WIDTH), 0.02),
        "conv_ln_b": nrm(ks[5], (L, CONV_WIDTH), 0.02),
        "w_conv_out": nrm(ks[6], (L, CONV_WIDTH, D), CONV_WIDTH ** -0.5),
        "w_pool": nrm(ks[7], (L, N_POOL_GROUPS, POOL_GROUP, POOL_OUT_GROUP), POOL_GROUP ** -0.5),
        "pool_scale": 1.0 + nrm(ks[8], (L, D), 0.02),
        "w_out": nrm(ks[9], (L, D, D), D ** -0.5 * DEEPNORM_BETA),
        "ln1_g": 1.0 + nrm(ks[10], (L, D), 0.02),
        "ln1_b": nrm(ks[11], (L, D), 0.02),
        "w_ffn_in": nrm(ks[12], (L, D, 2 * FFN_HIDDEN), D ** -0.5),
        "w_ffn_out": nrm(ks[13], (L, FFN_HIDDEN, D), FFN_HIDDEN ** -0.5 * DEEPNORM_BETA),
        "ln2_g": 1.0 + nrm(ks[14], (L, D), 0.02),
        "ln2_b": nrm(ks[15], (L, D), 0.02),
    }


def reference(x, w_in, dw_kernel, dw_bias, conv_ln_g, conv_ln_b, w_conv_out,
              w_pool, pool_scale, w_out, ln1_g, ln1_b, w_ffn_in, w_ffn_out,
              ln2_g, ln2_b):
    h = x
    c0 = 2 * CONV_WIDTH
    c1 = c0 + POOL_WIDTH
    for l in range(DEPTH):
        proj = jnp.einsum("bsd,de->bse", h, w_in[l])
        u_conv = proj[..., :c0]
        u_pool = proj[..., c0:c1]
        gates = jax.nn.sigmoid(proj[..., c1:])
        g_conv, g_pool = jnp.split(gates, N_BRANCHES, axis=-1)
        y_conv = conformer_conv_branch(u_conv, dw_kernel[l], dw_bias[l],
                                       conv_ln_g[l], conv_ln_b[l], w_conv_out[l])
        y_pool = multiscale_pool_branch(u_pool, w_pool[l], pool_scale[l])
        merged = g_conv * y_conv + g_pool * y_pool
        mix = jnp.einsum("bsd,de->bse", merged, w_out[l])
        h = layer_norm(DEEPNORM_ALPHA * h + mix, ln1_g[l], ln1_b[l])
        gu = jnp.einsum("bsd,df->bsf", h, w_ffn_in[l])
        g, up = jnp.split(gu, 2, axis=-1)
        f = jnp.einsum("bsf,fd->bsd", jax.nn.silu(g) * up, w_ffn_out[l])
        h = layer_norm(DEEPNORM_ALPHA * h + f, ln2_g[l], ln2_b[l])
    return h
```

```python
import math
import os
KSTOP = int(os.environ.get('KSTOP', '99'))
import numpy as np
import concourse.bass as bass
import concourse.mybir as mybir
from concourse.bass_utils import run_bass_kernel_spmd

F32 = mybir.dt.float32
BF16 = mybir.dt.bfloat16
AF = mybir.ActivationFunctionType
ALU = mybir.AluOpType

D = 2048
T = 512
NCORES = 8
CW = 1024
KT = 31
FH = 5632
NJ = FH // 128
HALO = 32
ALPHA = 2.0 ** 0.25
EPS = 1e-5
SLOT = 4096
NSLOT = 8 + 4 + 8 + 16 + 16 + 8 + 44 + 24
CP_DWB, CP_LNG, CP_LNB, CP_PSC, CP_CORR = 0, 8, 16, 24, 40
CP_EPS = 40 + 4 * 16
CP_G1 = CP_EPS + 1
CP_B1 = CP_G1 + 16
NCP = CP_B1 + 16


class SB:
    def __init__(self, nc, name, shape, dtype, off):
        self.t = nc.alloc_sbuf_tensor_at(name, list(shape), dtype, offset=off)
        self.off = off
        self.esz = 2 if dtype == BF16 else 4
        self.shape = list(shape)
        self.row = int(np.prod(shape[1:])) * self.esz
        self.chunk = (int(np.prod(shape[2:])) * self.esz) if len(shape) > 2 else self.row

    def reg(self, c0=None, c1=None, lo=None, hi=None):
        if c0 is None:
            return (self.off, self.off + self.row)
        if c1 is None:
            c1 = c0 + 1
        if lo is None:
            return (self.off + c0 * self.chunk, self.off + c1 * self.chunk)
        assert c1 == c0 + 1
        return (self.off + c0 * self.chunk + lo * self.esz, self.off + c0 * self.chunk + hi * self.esz)


class Sched:
    ENG = ["pe", "act", "dve", "pool", "sp"]

    def __init__(self, nc):
        self.nc = nc
        self.streams = {e: [] for e in self.ENG}
        self.seq = {e: 0 for e in self.ENG}
        self.semh = {e: nc.alloc_semaphore("sem_" + e) for e in ["pe", "act", "dve", "pool"]}
        self.known = {e: {} for e in self.ENG}
        self.w = {}
        self.r = {}
        self.dcnt = {}
        self.nwaits = 0

    def _atoms(self, regs):
        for rg in regs:
            if isinstance(rg[0], str):
                yield rg
            else:
                lo, hi = rg
                for a in range(lo >> 8, ((hi - 1) >> 8) + 1):
                    yield a

    def _deps(self, reads, writes):
        need = {}

        def add(k, v):
            if need.get(k, 0) < v:
                need[k] = v

        for a in self._atoms(reads):
            t = self.w.get(a)
            if t is not None:
                add(*t)
        for a in self._atoms(writes):
            t = self.w.get(a)
            if t is not None:
                add(*t)
            rr = self.r.get(a)
            if rr:
                for k, v in rr.items():
                    add(k, v)
        return need

    def _waits(self, eng, need):
        for k, v in need.items():
            if k == eng and eng == "pe":
                continue
            if self.known[eng].get(k, 0) >= v:
                continue
            self.known[eng][k] = v
            h = self.semh[k]
            self.nwaits += 1
            self.streams[eng].append(lambda e, h=h, v=v: e.wait_ge(h, v))

    def _update(self, tok, reads, writes):
        k, v = tok
        for a in self._atoms(writes):
            self.w[a] = tok
            self.r[a] = None
        for a in self._atoms(reads):
            rr = self.r.get(a)
            if rr is None:
                self.r[a] = {k: v}
            elif rr.get(k, 0) < v:
                rr[k] = v

    def op(self, eng, fn, reads=(), writes=()):
        reads = list(reads)
        writes = list(writes)
        self._waits(eng, self._deps(reads, writes))
        self.seq[eng] += 1
        v = self.seq[eng]
        h = self.semh[eng]
        self.streams[eng].append(lambda e, fn=fn, h=h: fn(e).then_inc(h, 1))
        self._update((eng, v), reads, writes)

    def dma(self, q, out, in_, semname, reads=(), writes=()):
        reads = list(reads)
        writes = list(writes)
        self._waits(q, self._deps(reads, writes))
        if semname not in self.semh:
            self.semh[semname] = self.nc.alloc_semaphore("sem_" + semname)
            self.dcnt[semname] = 0
        self.dcnt[semname] += 16
        v = self.dcnt[semname]
        h = self.semh[semname]
        self.streams[q].append(lambda e, out=out, in_=in_, h=h: e.dma_start(out=out, in_=in_).then_inc(h, 16))
        self._update((semname, v), reads, writes)
        return (semname, v)

    def wait_tok(self, eng, tok):
        self._waits(eng, {tok[0]: tok[1]})


def build_program(n_tiles):
    nc = bass.Bass("TRN2", target_bir_lowering=False)
    n_tok = n_tiles * T
    x_d = nc.dram_tensor("x", [n_tok, D], F32, kind="ExternalInput").ap()
    xh_d = nc.dram_tensor("xh", [HALO, D], F32, kind="ExternalInput").ap()
    wq_d = nc.dram_tensor("wq", [NSLOT, 128, SLOT], F32, kind="ExternalInput").ap()
    cp_d = nc.dram_tensor("cp", [128, NCP], F32, kind="ExternalInput").ap()
    gb_d = nc.dram_tensor("gb", [4, 128, D], F32, kind="ExternalInput").ap()
    mats_d = nc.dram_tensor("mats", [2, 128, 128], F32, kind="ExternalInput").ap()
    out_d = nc.dram_tensor("out", [n_tok, D], F32, kind="ExternalOutput").ap()

    S = Sched(nc)
    off = [((nc.sbuf_base + 255) // 256) * 256]

    def alloc(name, shape, dtype, at=None):
        if at is not None:
            return SB(nc, name, shape, dtype, at)
        b = SB(nc, name, shape, dtype, off[0])
        off[0] += ((b.row + 255) // 256) * 256
        return b

    cp = alloc("cp", [128, NCP], F32)
    ident = alloc("ident", [128, 128], F32)
    ident_bf = alloc("ident_bf", [128, 128], BF16)
    ones_bf = alloc("ones_bf", [128, 128], BF16)
    ones = alloc("ones", [128, 128], F32)
    halo_u = alloc("halo_u", [128, 8, HALO], BF16)
    halo_p = alloc("halo_p", [128, 8, HALO], F32)
    xTh = alloc("xTh", [128, 16, HALO], BF16)
    bnst = alloc("bnst", [128, 4, 4 * 6], F32)
    mv = alloc("mv", [128, 4, 2], F32)
    rs = alloc("rs", [128, 4, 2], F32)
    gb = alloc("gb", [128, 4, D], F32)
    xT = alloc("xT", [128, 16, T], BF16)
    mh_off = off[0]
    merged = alloc("merged", [128, 16, T], BF16)
    hT = alloc("hT", [128, 16, T], BF16, at=mh_off)
    acta_off = off[0]
    act = alloc("act", [128, NJ, T], BF16)
    pool_u = alloc("pool_u", [128, 8, HALO + T], F32, at=acta_off)
    pscr = [alloc("pscr%d" % i, [128, HALO + T], F32, at=acta_off + 17408 + i * 2176) for i in range(2)]
    NCV = 4
    cvring = [alloc("cvring%d" % i, [128, 1280], BF16, at=acta_off + i * 2560) for i in range(NCV)]
    cv_bf = alloc("cv_bf", [128, 8, T], BF16, at=acta_off + 21760)
    pooled_bf = alloc("pooled_bf", [128, 8, T], BF16, at=acta_off + 29952)
    tm_off = off[0]
    tm = alloc("tm", [128, 4, D], F32)
    v_sb = alloc("v_sb", [128, 8, T], F32, at=mh_off)
    xring = [alloc("xring%d" % i, [128, D], BF16) for i in range(4)]
    u_bf = alloc("u_bf", [128, 8, HALO + T], BF16, at=xring[0].off)
    NW = 4
    wring = [alloc("wring%d" % i, [128, SLOT], BF16) for i in range(NW)]
    mean_t = alloc("mean_t", [128, T], F32, at=acta_off + 38144)
    rstd_t = alloc("rstd_t", [128, T], F32, at=acta_off + 38144 + 2048)
    NSCR = 6
    scr = [alloc("scr%d" % i, [128, T], F32) for i in range(NSCR)]
    scr_bf = [alloc("scrbf%d" % i, [128, 2, T], BF16, at=scr[i].off) for i in range(NSCR)]
    assert off[0] <= nc.sbuf_top, (off[0], nc.sbuf_top)

    ps = [nc.alloc_psum_tensor("ps%d" % i, [128, 512], F32) for i in range(8)]
    psb = [p_.bitcast(BF16) for p_ in ps]
    st = {"bank": 0, "scr": 0, "w": 0, "aux": 0, "xr": 0, "cv": 0}

    def nb():
        b = st["bank"]
        st["bank"] = (b + 1) % 6
        return b

    def naux():
        b = 6 + st["aux"]
        st["aux"] ^= 1
        return b

    def nscr():
        i = st["scr"]
        st["scr"] = (i + 1) % NSCR
        return scr[i]

    def nscr_bf():
        i = st["scr"]
        st["scr"] = (i + 1) % NSCR
        return scr_bf[i]

    def P(b):
        return ("ps", b)

    S.dma("sp", cp.t[:, :], cp_d[:, :], "c0", writes=[cp.reg()])
    S.dma("sp", ident.t[:, :], mats_d[0, :, :], "c1", writes=[ident.reg()])
    S.dma("sp", ones.t[:, :], mats_d[1, :, :], "c2", writes=[ones.reg()])
    S.dma("pool", ident_bf.t[:, :], mats_d[0, :, :], "c4", writes=[ident_bf.reg()])
    S.dma("pool", ones_bf.t[:, :], mats_d[1, :, :], "c5", writes=[ones_bf.reg()])
    for i in range(4):
        S.dma("sp", gb.t[:, i, :], gb_d[i, :, :], "c3_%d" % i, writes=[gb.reg(i)])

    wslot_ctr = [0]

    def load_w(idx, nel=SLOT):
        k = st["w"]
        st["w"] = (k + 1) % NW
        wb = wring[k]
        S.dma("pool", wb.t[:, 0:nel], wq_d[idx, :, 0:nel], "w%d" % k, writes=[wb.reg()])
        return wb

    def cpc(col):
        return cp.t[:, col:col + 1]

    def stage0(x_src, nrow, dst, dst_cols):
        k = st["xr"]
        st["xr"] = (k + 1) % 4
        xb = xring[k]
        S.dma("pool", xb.t[0:nrow, :], x_src, "xr%d" % k, writes=[xb.reg()])
        for g in range(4):
            b = naux()

            def fn(e, g=g, b=b, xb=xb):
                ins = None
                for q in range(4):
                    kc = 4 * g + q
                    ins = e.transpose(psb[b][:, q * nrow:(q + 1) * nrow],
                                      xb.t[0:nrow, kc * 128:(kc + 1) * 128], ident_bf.t[0:nrow, 0:nrow])
                return ins
            S.op("pe", fn, reads=[xb.reg(), ident_bf.reg()], writes=[P(b)])
            src = psb[b][:, 0:4 * nrow].rearrange("p (q t) -> p q t", q=4)
            S.op("act", lambda e, src=src, g=g: e.activation(dst.t[:, 4 * g:4 * g + 4, dst_cols[0]:dst_cols[1]], src, AF.Copy),
                 reads=[P(b)], writes=[dst.reg(4 * g, 4 * g + 4)])

    def mm_group(e, b, n, pairs):
        ins = None
        L = len(pairs)
        for i, (lhsT, rhs) in enumerate(pairs):
            ins = e.matmul(ps[b][:, 0:n], lhsT, rhs, start=(i == 0), stop=(i == L - 1))
        return ins

    def stage1(xsrc, n, halo_pass, hooks=()):
        for m in range(4):
            wb = load_w(8 + m)
            for h in range(2):
                c = 2 * m + h
                b = nb()
                pairs = [(wb.t[:, kc * 256 + h * 128: kc * 256 + h * 128 + 128], xsrc.t[:, kc, 0:n]) for kc in range(16)]
                S.op("pe", lambda e, b=b, pairs=pairs: mm_group(e, b, n, pairs),
                     reads=[wb.reg(), xsrc.reg()], writes=[P(b)])
                if halo_pass:
                    dst, dreg = halo_p.t[:, c, :], halo_p.reg(c)
                else:
                    dst, dreg = pool_u.t[:, c, HALO:HALO + n], pool_u.reg(c, None, HALO, HALO + n)
                S.op("act", lambda e, dst=dst, b=b: e.activation(dst, ps[b][:, 0:n], AF.Copy),
                     reads=[P(b)], writes=[dreg])
        for j in range(8):
            wb = load_w(j)
            A, G = nb(), nb()
            for h, b in ((0, A), (1, G)):
                pairs = [(wb.t[:, kc * 256 + h * 128: kc * 256 + h * 128 + 128], xsrc.t[:, kc, 0:n]) for kc in range(16)]
                S.op("pe", lambda e, b=b, pairs=pairs: mm_group(e, b, n, pairs),
                     reads=[wb.reg(), xsrc.reg()], writes=[P(b)])
            sg = nscr()
            S.op("act", lambda e, sg=sg, G=G: e.activation(sg.t[:, 0:n], ps[G][:, 0:n], AF.Sigmoid),
                 reads=[P(G)], writes=[sg.reg()])
            if halo_pass:
                dst, dreg = halo_u.t[:, j, :], halo_u.reg(j)
            else:
                dst, dreg = u_bf.t[:, j, HALO:HALO + n], u_bf.reg(j, None, HALO, HALO + n)
            S.op("dve", lambda e, dst=dst, A=A, sg=sg: e.tensor_tensor(dst, ps[A][:, 0:n], sg.t[:, 0:n], ALU.mult),
                 reads=[P(A), sg.reg()], writes=[dreg])
            if hooks and j % 2 == 1:
                hooks[j // 2]()

    def pool_windows(first_tile):
        for c in range(8):
            g = c // 2
            w = 2 ** (g + 1)
            cur, cur_reg = (lambda lo, hi, c=c: pool_u.t[:, c, lo:hi]), pool_u.reg(c)
            lo = 0
            for l in range(1, g + 2):
                step = 2 ** (l - 1)
                nlo = lo + step
                ob = pscr[(l - 1) % 2]
                S.op("dve", lambda e, ob=ob, cur=cur, nlo=nlo, step=step:
                     e.tensor_tensor(ob.t[:, nlo:HALO + T], cur(nlo, HALO + T), cur(nlo - step, HALO + T - step), ALU.add),
                     reads=[cur_reg], writes=[ob.reg()])
                cur, cur_reg = (lambda lo_, hi_, ob=ob: ob.t[:, lo_:hi_]), ob.reg()
                lo = nlo
            if first_tile:
                S.op("dve", lambda e, cur=cur, g=g: e.tensor_tensor(cur(HALO, HALO + 16), cur(HALO, HALO + 16),
                                                                   cp.t[:, CP_CORR + 16 * g: CP_CORR + 16 * g + 16], ALU.mult),
                     reads=[cur_reg, cp.reg()], writes=[cur_reg])
            S.op("dve", lambda e, cur=cur, c=c, w=w:
                 e.scalar_tensor_tensor(pooled_bf.t[:, c, :], cur(HALO, HALO + T), 1.0 / w, pool_u.t[:, c, HALO:HALO + T],
                                        ALU.mult, ALU.subtract),
                 reads=[cur_reg, pool_u.reg(c)], writes=[pooled_bf.reg(c)])

    def stage2(first_tile):
        pend = []

        def stats_mm(j, hv, hs):
            for (bank, h) in ((6, hv), (7, hs)):
                def fn(e, bank=bank, h=h, j=j):
                    e.matmul(ps[bank][:, :], ones_bf.t[:, :], h.t[:, 0, :], start=(j == 0), stop=False)
                    return e.matmul(ps[bank][:, :], ones_bf.t[:, :], h.t[:, 1, :], start=False, stop=(j == 7))
                S.op("pe", fn, reads=[ones_bf.reg(), h.reg()], writes=[P(bank)])

        for j in range(8):
            wb = load_w(12 + j, KT * 128)
            V = nb()
            pairs = [(wb.t[:, k * 128:(k + 1) * 128], u_bf.t[:, j, 2 + k: 2 + k + T]) for k in range(KT)]
            S.op("pe", lambda e, V=V, pairs=pairs: mm_group(e, V, T, pairs), reads=[wb.reg(), u_bf.reg(j)], writes=[P(V)])
            S.op("act", lambda e, V=V, j=j: e.activation(v_sb.t[:, j, :], ps[V][:, :], AF.Identity, bias=cpc(CP_DWB + j)),
                 reads=[P(V), cp.reg()], writes=[v_sb.reg(j)])
            sq = nscr()
            S.op("act", lambda e, V=V, j=j, sq=sq: e.activation(sq.t[:, :], ps[V][:, :], AF.Square, bias=cpc(CP_DWB + j)),
                 reads=[P(V), cp.reg()], writes=[sq.reg()])
            hv, hs = nscr_bf(), nscr_bf()
            for (dst, srcap, srcreg) in ((hv, v_sb.t[:, j, :], v_sb.reg(j)), (hs, sq.t[:, :], sq.reg())):
                S.op("dve", lambda e, dst=dst, srcap=srcap: e.tensor_copy(dst.t[:, 0, :], srcap), reads=[srcreg], writes=[dst.reg(0)])
                S.op("dve", lambda e, dst=dst, srcap=srcap: e.tensor_tensor(dst.t[:, 1, :], srcap, dst.t[:, 0, :], ALU.subtract),
                     reads=[srcreg, dst.reg(0)], writes=[dst.reg(1)])
            pend.append((j, hv, hs))
            if len(pend) > 1:
                stats_mm(*pend.pop(0))
            if j == 3:
                pool_windows(first_tile)
        while pend:
            stats_mm(*pend.pop(0))
        S.op("dve", lambda e: e.tensor_copy(mean_t.t[:, :], ps[6][:, :]), reads=[P(6)], writes=[mean_t.reg()])
        msq = nscr()
        S.op("dve", lambda e, msq=msq: e.tensor_tensor(msq.t[:, :], mean_t.t[:, :], mean_t.t[:, :], ALU.mult),
             reads=[mean_t.reg()], writes=[msq.reg()])
        S.op("dve", lambda e, msq=msq: e.tensor_tensor(rstd_t.t[:, :], ps[7][:, :], msq.t[:, :], ALU.subtract),
             reads=[P(7), msq.reg()], writes=[rstd_t.reg()])
        S.op("act", lambda e: e.activation(rstd_t.t[:, :], rstd_t.t[:, :], AF.Sqrt, bias=cpc(CP_EPS)),
             reads=[rstd_t.reg(), cp.reg()], writes=[rstd_t.reg()])
        S.op("dve", lambda e: e.reciprocal(rstd_t.t[:, :], rstd_t.t[:, :]),
             reads=[rstd_t.reg()], writes=[rstd_t.reg()])
        for j in range(8):
            y = nscr()
            S.op("dve", lambda e, y=y, j=j: e.tensor_tensor(y.t[:, :], v_sb.t[:, j, :], mean_t.t[:, :], ALU.subtract),
                 reads=[v_sb.reg(j), mean_t.reg()], writes=[y.reg()])
            S.op("dve", lambda e, y=y: e.tensor_tensor(y.t[:, :], y.t[:, :], rstd_t.t[:, :], ALU.mult),
                 reads=[y.reg(), rstd_t.reg()], writes=[y.reg()])
            S.op("act", lambda e, y=y, j=j: e.activation(cv_bf.t[:, j, :], y.t[:, :], AF.Silu,
                                                        bias=cpc(CP_LNB + j), scale=cpc(CP_LNG + j)),
                 reads=[y.reg(), cp.reg()], writes=[cv_bf.reg(j)])

    def stage3():
        LAG = 2
        live = {}

        def front(j):
            gw = load_w(21 + 2 * j)
            kcv = st["cv"]
            st["cv"] = (kcv + 1) % NCV
            cvp = cvring[kcv]
            S.dma("pool", cvp.t[:, :], wq_d[20 + 2 * j, :, 0:1280], "cv%d" % kcv, writes=[cvp.reg()])
            GC, GP, YP = nb(), nb(), nb()
            for h, b in ((0, GC), (1, GP)):
                pairs = [(gw.t[:, kc * 256 + h * 128: kc * 256 + h * 128 + 128], xT.t[:, kc, :]) for kc in range(16)]
                S.op("pe", lambda e, b=b, pairs=pairs: mm_group(e, b, T, pairs), reads=[gw.reg(), xT.reg()], writes=[P(b)])
            g = j // 4
            pairs = [(cvp.t[:, 1024 + k2 * 128: 1024 + k2 * 128 + 128], pooled_bf.t[:, 2 * g + k2, :]) for k2 in range(2)]
            S.op("pe", lambda e, b=YP, pairs=pairs: mm_group(e, b, T, pairs),
                 reads=[cvp.reg(), pooled_bf.reg(2 * g, 2 * g + 2)], writes=[P(YP)])
            sc, sp_ = nscr(), nscr()
            S.op("act", lambda e, sc=sc, GC=GC: e.activation(sc.t[:, :], ps[GC][:, :], AF.Sigmoid), reads=[P(GC)], writes=[sc.reg()])
            S.op("act", lambda e, sp_=sp_, GP=GP: e.activation(sp_.t[:, :], ps[GP][:, :], AF.Sigmoid), reads=[P(GP)], writes=[sp_.reg()])
            S.op("dve", lambda e, sp_=sp_, YP=YP, j=j: e.scalar_tensor_tensor(sp_.t[:, :], ps[YP][:, :], cpc(CP_PSC + j), sp_.t[:, :],
                                                                             ALU.mult, ALU.mult),
                 reads=[P(YP), sp_.reg(), cp.reg()], writes=[sp_.reg()])
            live[j] = (cvp, sc, sp_)

        def back(j):
            cvp, sc, sp_ = live.pop(j)
            YC = nb()
            pairs = [(cvp.t[:, kc * 128: kc * 128 + 128], cv_bf.t[:, kc, :]) for kc in range(8)]
            S.op("pe", lambda e, b=YC, pairs=pairs: mm_group(e, b, T, pairs), reads=[cvp.reg(), cv_bf.reg()], writes=[P(YC)])
            S.op("dve", lambda e, sc=sc, YC=YC: e.tensor_tensor(sc.t[:, :], ps[YC][:, :], sc.t[:, :], ALU.mult),
                 reads=[P(YC), sc.reg()], writes=[sc.reg()])
            S.op("dve", lambda e, sc=sc, sp_=sp_, j=j: e.tensor_tensor(merged.t[:, j, :], sc.t[:, :], sp_.t[:, :], ALU.add),
                 reads=[sc.reg(), sp_.reg()], writes=[merged.reg(j)])

        for j in range(16 + LAG):
            if j < 16:
                front(j)
            if j - LAG >= 0:
                back(j - LAG)

    def ln_stats(s, q):
        S.op("dve", lambda e: e.bn_stats(bnst.t[:, s, q * 6:(q + 1) * 6], tm.t[:, s, q * 512:(q + 1) * 512]),
             reads=[tm.reg(s, None, q * 512, (q + 1) * 512)], writes=[bnst.reg(s, None, q * 6, (q + 1) * 6)])

    def ln_norm(s, par, zdst=None):
        S.op("dve", lambda e: e.bn_aggr(mv.t[:, par, :], bnst.t[:, s, :].rearrange("p (q f) -> p q f", q=4)),
             reads=[bnst.reg(s)], writes=[mv.reg(par)])
        S.op("act", lambda e: e.activation(rs.t[:, par, 0:1], mv.t[:, par, 1:2], AF.Sqrt, bias=cpc(CP_EPS)),
             reads=[mv.reg(par), cp.reg()], writes=[rs.reg(par)])
        S.op("dve", lambda e: e.reciprocal(rs.t[:, par, 0:1], rs.t[:, par, 0:1]),
             reads=[rs.reg(par)], writes=[rs.reg(par)])
        S.op("dve", lambda e: e.scalar_tensor_tensor(rs.t[:, par, 1:2], mv.t[:, par, 0:1], -1.0, rs.t[:, par, 0:1], ALU.mult, ALU.mult),
             reads=[mv.reg(par), rs.reg(par)], writes=[rs.reg(par)])
        if zdst is None:
            S.op("act", lambda e: e.activation(tm.t[:, s, :], tm.t[:, s, :], AF.Identity, bias=rs.t[:, par, 1:2], scale=rs.t[:, par, 0:1]),
                 reads=[tm.reg(s), rs.reg(par)], writes=[tm.reg(s)])
        else:
            S.op("act", lambda e: e.activation(zdst.t[:, :], tm.t[:, s, :], AF.Identity, bias=rs.t[:, par, 1:2], scale=rs.t[:, par, 0:1]),
                 reads=[tm.reg(s), rs.reg(par)], writes=[zdst.reg()])

    def ln_affine(s, gi):
        S.op("dve", lambda e: e.tensor_tensor(tm.t[:, s, :], tm.t[:, s, :], gb.t[:, gi, :], ALU.mult),
             reads=[tm.reg(s), gb.reg(gi)], writes=[tm.reg(s)])
        S.op("dve", lambda e: e.tensor_tensor(tm.t[:, s, :], tm.t[:, s, :], gb.t[:, gi + 1, :], ALU.add),
             reads=[tm.reg(s), gb.reg(gi + 1)], writes=[tm.reg(s)])

    def layernorm(s, gi, par):
        ln_norm(s, par)
        ln_affine(s, gi)

    def tokmajor_matmul(src, nk, slot0, nkg):
        for n in range(4):
            B = [nb() for _ in range(4)]
            for kg in range(nkg):
                k0 = kg * 8
                k1 = min(nk, k0 + 8)
                wb = load_w(slot0 + n * nkg + kg, (k1 - k0) * 512)

                def fn(e, wb=wb, k0=k0, k1=k1, B=B):
                    ins = None
                    for kc in range(k0, k1):
                        for s in range(4):
                            ins = e.matmul(ps[B[s]][:, :], src.t[:, kc, s * 128:(s + 1) * 128],
                                           wb.t[:, (kc - k0) * 512:(kc - k0 + 1) * 512], start=(kc == 0), stop=(kc == nk - 1))
                    return ins
                S.op("pe", fn, reads=[wb.reg(), src.reg(k0, k1)], writes=[P(b) for b in B])
            for s in range(4):
                S.op("dve", lambda e, s=s, n=n, B=B: e.scalar_tensor_tensor(tm.t[:, s, n * 512:(n + 1) * 512], tm.t[:, s, n * 512:(n + 1) * 512],
                                                                           ALPHA, ps[B[s]][:, :], ALU.mult, ALU.add),
                     reads=[P(B[s]), tm.reg(s, None, n * 512, (n + 1) * 512)], writes=[tm.reg(s, None, n * 512, (n + 1) * 512)])
                ln_stats(s, n)

    out_toks = []

    def do_stage0(ti):
        for s in range(4):
            stage0(x_d[ti * T + s * 128: ti * T + (s + 1) * 128, :], 128, xT, (s * 128, (s + 1) * 128))

    def tile_body(ti, hooks):
        tok0 = ti * T
        S.op("dve", lambda e: e.tensor_copy(u_bf.t[:, :, 0:HALO], halo_u.t[:, :, :]), reads=[halo_u.reg()], writes=[u_bf.reg()])
        S.op("dve", lambda e: e.tensor_copy(pool_u.t[:, :, 0:HALO], halo_p.t[:, :, :]), reads=[halo_p.reg()], writes=[pool_u.reg()])
        if KSTOP < 1:
            return
        stage1(xT, T, False, hooks)
        S.op("dve", lambda e: e.tensor_copy(halo_u.t[:, :, :], u_bf.t[:, :, T:T + HALO]), reads=[u_bf.reg()], writes=[halo_u.reg()])
        S.op("dve", lambda e: e.tensor_copy(halo_p.t[:, :, :], pool_u.t[:, :, T:T + HALO]), reads=[pool_u.reg()], writes=[halo_p.reg()])
        if KSTOP < 2:
            return
        stage2(ti == 0)
        if KSTOP < 3:
            return
        stage3()
        if KSTOP < 4:
            return
        for s in range(4):
            S.dma("sp", tm.t[:, s, :], x_d[tok0 + s * 128: tok0 + (s + 1) * 128, :], "tm%d" % s, writes=[tm.reg(s)])
        tokmajor_matmul(merged, 16, 52, 2)
        for s in range(4):
            zb = xring[s]
            ln_norm(s, s, zdst=zb)
            for g in range(4):
                b = naux()

                def fn(e, g=g, b=b, zb=zb):
                    ins = None
                    for q in range(4):
                        kc = 4 * g + q
                        ins = e.transpose(psb[b][:, q * 128:(q + 1) * 128], zb.t[:, kc * 128:(kc + 1) * 128], ident_bf.t[:, :])
                    return ins
                S.op("pe", fn, reads=[zb.reg(), ident_bf.reg()], writes=[P(b)])
                for q in range(4):
                    kc = 4 * g + q
                    if g % 2 == 0:
                        S.op("act", lambda e, b=b, q=q, kc=kc, s=s: e.activation(hT.t[:, kc, s * 128:(s + 1) * 128], psb[b][:, q * 128:(q + 1) * 128],
                                                                               AF.Identity, bias=cpc(CP_B1 + kc), scale=cpc(CP_G1 + kc)),
                             reads=[P(b), cp.reg()], writes=[hT.reg(kc, None, s * 128, (s + 1) * 128)])
                    else:
                        S.op("dve", lambda e, b=b, q=q, kc=kc, s=s: e.tensor_scalar(hT.t[:, kc, s * 128:(s + 1) * 128], psb[b][:, q * 128:(q + 1) * 128],
                                                                                  cpc(CP_G1 + kc), cpc(CP_B1 + kc), ALU.mult, ALU.add),
                             reads=[P(b), cp.reg()], writes=[hT.reg(kc, None, s * 128, (s + 1) * 128)])
        for s in range(4):
            S.op("dve", lambda e, s=s: e.tensor_scalar(tm.t[:, s, :], tm.t[:, s, :], rs.t[:, s, 0:1], rs.t[:, s, 1:2], ALU.mult, ALU.add),
                 reads=[tm.reg(s), rs.reg(s)], writes=[tm.reg(s)])
            ln_affine(s, 0)
        if KSTOP < 6:
            return
        for j in range(NJ):
            wb = load_w(60 + j)
            G, U = nb(), nb()
            for h, b in ((0, G), (1, U)):
                pairs = [(wb.t[:, kc * 256 + h * 128: kc * 256 + h * 128 + 128], hT.t[:, kc, :]) for kc in range(16)]
                S.op("pe", lambda e, b=b, pairs=pairs: mm_group(e, b, T, pairs), reads=[wb.reg(), hT.reg()], writes=[P(b)])
            sg = nscr()
            S.op("act", lambda e, sg=sg, G=G: e.activation(sg.t[:, :], ps[G][:, :], AF.Silu), reads=[P(G)], writes=[sg.reg()])
            S.op("dve", lambda e, sg=sg, U=U, j=j: e.tensor_tensor(act.t[:, j, :], ps[U][:, :], sg.t[:, :], ALU.mult),
                 reads=[P(U), sg.reg()], writes=[act.reg(j)])
        if KSTOP < 7:
            return
        tokmajor_matmul(act, NJ, 104, 6)
        if ti + 1 < n_tiles:
            do_stage0(ti + 1)

    def ln2_hooks(ti):
        def mk(s):
            def h():
                ln_norm(s, s)
                ln_affine(s, 2)
                tok = S.dma("sp", out_d[ti * T + s * 128: ti * T + (s + 1) * 128, :], tm.t[:, s, :], "out%d" % s, reads=[tm.reg(s)])
                out_toks.append(tok)
            return h
        return [mk(s) for s in range(4)]

    stage0(xh_d[:, :], HALO, xTh, (0, HALO))
    stage1(xTh, HALO, True)
    do_stage0(0)
    hooks = ()
    for ti in range(n_tiles):
        tile_body(ti, hooks)
        hooks = ln2_hooks(ti)
    for h in hooks:
        h()
    for tok in out_toks[-4:]:
        S.wait_tok("sp", tok)
    if not out_toks:
        tok = S.dma("sp", out_d[0:128, :], gb.t[:, 0, :], "out0", reads=[gb.reg(0)])
        S.wait_tok("sp", tok)
    with nc.Block() as block:
        @block.tensor
        def _(e):
            for f in S.streams["pe"]:
                f(e)

        @block.scalar
        def _(e):
            for f in S.streams["act"]:
                f(e)

        @block.vector
        def _(e):
            for f in S.streams["dve"]:
                f(e)

        @block.gpsimd
        def _(e):
            for f in S.streams["pool"]:
                f(e)

        @block.sync
        def _(e):
            for f in S.streams["sp"]:
                f(e)
    return nc


def pack_weights(w_in, dw_kernel, w_conv_out, w_pool, w_out, w_ffn_in, w_ffn_out):
    wq = np.zeros((NSLOT, 128, SLOT), np.float32)

    def pair16(W, cols):
        sub = W[:, cols]
        return sub.reshape(16, 128, 256).transpose(1, 0, 2).reshape(128, 4096)

    ar = np.arange(128)
    i = 0
    for j in range(8):
        wq[i] = pair16(w_in, np.concatenate([j * 128 + ar, 1024 + j * 128 + ar])); i += 1
    for m in range(4):
        wq[i] = pair16(w_in, 2048 + m * 256 + np.arange(256)); i += 1
    for j in range(8):
        blk = np.zeros((128, KT, 128), np.float32)
        blk[ar, :, ar] = dw_kernel[:, j * 128:(j + 1) * 128].T
        wq[i, :, :KT * 128] = blk.reshape(128, KT * 128); i += 1
    for j in range(16):
        cvo = w_conv_out[:, j * 128:(j + 1) * 128].reshape(8, 128, 128).transpose(1, 0, 2).reshape(128, 1024)
        g = j // 4
        pw = w_pool[g][:, (j % 4) * 128:(j % 4 + 1) * 128].reshape(2, 128, 128).transpose(1, 0, 2).reshape(128, 256)
        wq[i, :, 0:1024] = cvo
        wq[i, :, 1024:1280] = pw
        i += 1
        wq[i] = pair16(w_in, np.concatenate([3072 + j * 128 + ar, 5120 + j * 128 + ar])); i += 1
    for n in range(4):
        for kg in range(2):
            blk = w_out[kg * 1024:(kg + 1) * 1024, n * 512:(n + 1) * 512].reshape(8, 128, 512).transpose(1, 0, 2)
            wq[i] = blk.reshape(128, 4096); i += 1
    for j in range(NJ):
        wq[i] = pair16(w_ffn_in, np.concatenate([j * 128 + ar, FH + j * 128 + ar])); i += 1
    for n in range(4):
        for kg in range(6):
            k0 = kg * 8
            k1 = min(NJ, k0 + 8)
            blk = w_ffn_out[k0 * 128:k1 * 128, n * 512:(n + 1) * 512].reshape(k1 - k0, 128, 512).transpose(1, 0, 2)
            wq[i, :, :(k1 - k0) * 512] = blk.reshape(128, (k1 - k0) * 512); i += 1
    assert i == NSLOT
    return wq


def pack_consts(dw_bias, conv_ln_g, conv_ln_b, pool_scale, ln1_g, ln1_b):
    cp = np.ones((128, NCP), np.float32)
    cp[:, CP_DWB:CP_DWB + 8] = dw_bias.reshape(8, 128).T
    cp[:, CP_LNG:CP_LNG + 8] = conv_ln_g.reshape(8, 128).T
    cp[:, CP_LNB:CP_LNB + 8] = conv_ln_b.reshape(8, 128).T
    cp[:, CP_PSC:CP_PSC + 16] = pool_scale.reshape(16, 128).T
    cp[:, CP_EPS] = EPS
    cp[:, CP_G1:CP_G1 + 16] = ln1_g.reshape(16, 128).T
    cp[:, CP_B1:CP_B1 + 16] = ln1_b.reshape(16, 128).T
    return cp


def start_corr():
    c = np.ones((128, 4 * 16), np.float32)
    for g in range(4):
        w = 2 ** (g + 1)
        for t in range(16):
            c[:, g * 16 + t] = w / min(t + 1, w)
    return c


_PROGRAMS = {}


def run_cores(x_rows_list, halo_list, start_flags, wq, cp, gbb, n_tiles):
    if n_tiles not in _PROGRAMS:
        _PROGRAMS[n_tiles] = build_program(n_tiles)
    nc = _PROGRAMS[n_tiles]
    mats = np.stack([np.eye(128, dtype=np.float32), np.full((128, 128), 1.0 / CW, np.float32)])
    corr = start_corr()
    in_maps = []
    for xr, xh, sf in zip(x_rows_list, halo_list, start_flags):
        cpc = cp.copy()
        if sf:
            cpc[:, CP_CORR:CP_CORR + 64] = corr
        in_maps.append({"x": np.ascontiguousarray(xr), "xh": np.ascontiguousarray(xh), "wq": wq, "cp": cpc, "gb": gbb, "mats": mats})
    res = run_bass_kernel_spmd(nc, in_maps, core_ids=list(range(len(in_maps))))
    return [r["out"] for r in res.results]


def kernel(x, w_in, dw_kernel, dw_bias, conv_ln_g, conv_ln_b, w_conv_out, w_pool, pool_scale, w_out,
           ln1_g, ln1_b, w_ffn_in, w_ffn_out, ln2_g, ln2_b):
    x = np.asarray(x, np.float32)
    B, SEQ, _ = x.shape
    f = lambda a: np.asarray(a, np.float32)[0]
    wq = pack_weights(f(w_in), f(dw_kernel), f(w_conv_out), f(w_pool), f(w_out), f(w_ffn_in), f(w_ffn_out))
    cp = pack_consts(f(dw_bias), f(conv_ln_g), f(conv_ln_b), f(pool_scale), f(ln1_g), f(ln1_b))
    gbb = np.stack([np.broadcast_to(f(a)[None, :], (128, D)) for a in (ln1_g, ln1_b, ln2_g, ln2_b)]).astype(np.float32)
    per_core = (B * SEQ) // NCORES
    n_tiles = per_core // T
    xs, hs, sf = [], [], []
    xf = x.reshape(B * SEQ, D)
    for c in range(NCORES):
        r0 = c * per_core
        xs.append(xf[r0:r0 + per_core])
        if r0 % SEQ == 0:
            hs.append(np.zeros((HALO, D), np.float32)); sf.append(True)
        else:
            hs.append(xf[r0 - HALO:r0]); sf.append(False)
    outs = run_cores(xs, hs, sf, wq, cp, gbb, n_tiles)
    return np.concatenate(outs, axis=0).reshape(B, SEQ, D).astype(np.float32)
```

```python
import math
import os
KSTOP = int(os.environ.get('KSTOP', '99'))
import numpy as np
import concourse.bass as bass
import concourse.mybir as mybir
from concourse.bass_utils import run_bass_kernel_spmd

F32 = mybir.dt.float32
BF16 = mybir.dt.bfloat16
AF = mybir.ActivationFunctionType
ALU = mybir.AluOpType

D = 2048
T = 512
NCORES = 8
CW = 1024
KT = 31
FH = 5632
NJ = FH // 128
HALO = 32
ALPHA = 2.0 ** 0.25
EPS = 1e-5
SLOT = 4096
NSLOT = 8 + 4 + 8 + 16 + 16 + 8 + 44 + 24
CP_DWB, CP_LNG, CP_LNB, CP_PSC, CP_CORR = 0, 8, 16, 24, 40
CP_EPS = 40 + 4 * 16
CP_G1 = CP_EPS + 1
CP_B1 = CP_G1 + 16
NCP = CP_B1 + 16


class SB:
    def __init__(self, nc, name, shape, dtype, off):
        self.t = nc.alloc_sbuf_tensor_at(name, list(shape), dtype, offset=off)
        self.off = off
        self.esz = 2 if dtype == BF16 else 4
        self.shape = list(shape)
        self.row = int(np.prod(shape[1:])) * self.esz
        self.chunk = (int(np.prod(shape[2:])) * self.esz) if len(shape) > 2 else self.row

    def reg(self, c0=None, c1=None, lo=None, hi=None):
        if c0 is None:
            return (self.off, self.off + self.row)
        if c1 is None:
            c1 = c0 + 1
        if lo is None:
            return (self.off + c0 * self.chunk, self.off + c1 * self.chunk)
        assert c1 == c0 + 1
        return (self.off + c0 * self.chunk + lo * self.esz, self.off + c0 * self.chunk + hi * self.esz)


class Sched:
    ENG = ["pe", "act", "dve", "pool", "sp"]

    def __init__(self, nc):
        self.nc = nc
        self.streams = {e: [] for e in self.ENG}
        self.seq = {e: 0 for e in self.ENG}
        self.semh = {e: nc.alloc_semaphore("sem_" + e) for e in ["pe", "act", "dve", "pool"]}
        self.known = {e: {} for e in self.ENG}
        self.w = {}
        self.r = {}
        self.dcnt = {}
        self.nwaits = 0

    def _atoms(self, regs):
        for rg in regs:
            if isinstance(rg[0], str):
                yield rg
            else:
                lo, hi = rg
                for a in range(lo >> 8, ((hi - 1) >> 8) + 1):
                    yield a

    def _deps(self, reads, writes):
        need = {}

        def add(k, v):
            if need.get(k, 0) < v:
                need[k] = v

        for a in self._atoms(reads):
            t = self.w.get(a)
            if t is not None:
                add(*t)
        for a in self._atoms(writes):
            t = self.w.get(a)
            if t is not None:
                add(*t)
            rr = self.r.get(a)
            if rr:
                for k, v in rr.items():
                    add(k, v)
        return need

    def _waits(self, eng, need):
        for k, v in need.items():
            if k == eng and eng == "pe":
                continue
            if self.known[eng].get(k, 0) >= v:
                continue
            self.known[eng][k] = v
            h = self.semh[k]
            self.nwaits += 1
            self.streams[eng].append(lambda e, h=h, v=v: e.wait_ge(h, v))

    def _update(self, tok, reads, writes):
        k, v = tok
        for a in self._atoms(writes):
            self.w[a] = tok
            self.r[a] = None
        for a in self._atoms(reads):
            rr = self.r.get(a)
            if rr is None:
                self.r[a] = {k: v}
            elif rr.get(k, 0) < v:
                rr[k] = v

    def op(self, eng, fn, reads=(), writes=()):
        reads = list(reads)
        writes = list(writes)
        self._waits(eng, self._deps(reads, writes))
        self.seq[eng] += 1
        v = self.seq[eng]
        h = self.semh[eng]
        self.streams[eng].append(lambda e, fn=fn, h=h: fn(e).then_inc(h, 1))
        self._update((eng, v), reads, writes)

    def dma(self, q, out, in_, semname, reads=(), writes=()):
        reads = list(reads)
        writes = list(writes)
        self._waits(q, self._deps(reads, writes))
        if semname not in self.semh:
            self.semh[semname] = self.nc.alloc_semaphore("sem_" + semname)
            self.dcnt[semname] = 0
        self.dcnt[semname] += 16
        v = self.dcnt[semname]
        h = self.semh[semname]
        self.streams[q].append(lambda e, out=out, in_=in_, h=h: e.dma_start(out=out, in_=in_).then_inc(h, 16))
        self._update((semname, v), reads, writes)
        return (semname, v)

    def wait_tok(self, eng, tok):
        self._waits(eng, {tok[0]: tok[1]})


def build_program(n_tiles):
    nc = bass.Bass("TRN2", target_bir_lowering=False)
    n_tok = n_tiles * T
    x_d = nc.dram_tensor("x", [n_tok, D], F32, kind="ExternalInput").ap()
    xh_d = nc.dram_tensor("xh", [HALO, D], F32, kind="ExternalInput").ap()
    wq_d = nc.dram_tensor("wq", [NSLOT, 128, SLOT], F32, kind="ExternalInput").ap()
    cp_d = nc.dram_tensor("cp", [128, NCP], F32, kind="ExternalInput").ap()
    gb_d = nc.dram_tensor("gb", [4, 128, D], F32, kind="ExternalInput").ap()
    mats_d = nc.dram_tensor("mats", [2, 128, 128], F32, kind="ExternalInput").ap()
    out_d = nc.dram_tensor("out", [n_tok, D], F32, kind="ExternalOutput").ap()

    S = Sched(nc)
    off = [((nc.sbuf_base + 255) // 256) * 256]

    def alloc(name, shape, dtype, at=None):
        if at is not None:
            return SB(nc, name, shape, dtype, at)
        b = SB(nc, name, shape, dtype, off[0])
        off[0] += ((b.row + 255) // 256) * 256
        return b

    cp = alloc("cp", [128, NCP], F32)
    ident = alloc("ident", [128, 128], F32)
    ident_bf = alloc("ident_bf", [128, 128], BF16)
    ones_bf = alloc("ones_bf", [128, 128], BF16)
    ones = alloc("ones", [128, 128], F32)
    halo_u = alloc("halo_u", [128, 8, HALO], BF16)
    halo_p = alloc("halo_p", [128, 8, HALO], F32)
    xTh = alloc("xTh", [128, 16, HALO], BF16)
    bnst = alloc("bnst", [128, 4, 4 * 6], F32)
    mv = alloc("mv", [128, 4, 2], F32)
    rs = alloc("rs", [128, 4, 2], F32)
    gb = alloc("gb", [128, 4, D], F32)
    xT = alloc("xT", [128, 16, T], BF16)
    mh_off = off[0]
    merged = alloc("merged", [128, 16, T], BF16)
    hT = alloc("hT", [128, 16, T], BF16, at=mh_off)
    acta_off = off[0]
    act = alloc("act", [128, NJ, T], BF16)
    pool_u = alloc("pool_u", [128, 8, HALO + T], F32, at=acta_off)
    pscr = [alloc("pscr%d" % i, [128, HALO + T], F32, at=acta_off + 17408 + i * 2176) for i in range(2)]
    NCV = 4
    cvring = [alloc("cvring%d" % i, [128, 1280], BF16, at=acta_off + i * 2560) for i in range(NCV)]
    cv_bf = alloc("cv_bf", [128, 8, T], BF16, at=acta_off + 21760)
    pooled_bf = alloc("pooled_bf", [128, 8, T], BF16, at=acta_off + 29952)
    tm_off = off[0]
    tm = alloc("tm", [128, 4, D], F32)
    v_sb = alloc("v_sb", [128, 8, T], F32, at=mh_off)
    xring = [alloc("xring%d" % i, [128, D], BF16) for i in range(4)]
    u_bf = alloc("u_bf", [128, 8, HALO + T], BF16, at=xring[0].off)
    NW = 4
    wring = [alloc("wring%d" % i, [128, SLOT], BF16) for i in range(NW)]
    mean_t = alloc("mean_t", [128, T], F32, at=acta_off + 38144)
    rstd_t = alloc("rstd_t", [128, T], F32, at=acta_off + 38144 + 2048)
    NSCR = 6
    scr = [alloc("scr%d" % i, [128, T], F32) for i in range(NSCR)]
    scr_bf = [alloc("scrbf%d" % i, [128, 2, T], BF16, at=scr[i].off) for i in range(NSCR)]
    assert off[0] <= nc.sbuf_top, (off[0], nc.sbuf_top)

    ps = [nc.alloc_psum_tensor("ps%d" % i, [128, 512], F32) for i in range(8)]
    psb = [p_.bitcast(BF16) for p_ in ps]
    st = {"bank": 0, "scr": 0, "w": 0, "aux": 0, "xr": 0, "cv": 0}

    def nb():
        b = st["bank"]
        st["bank"] = (b + 1) % 6
        return b

    def naux():
        b = 6 + st["aux"]
        st["aux"] ^= 1
        return b

    def nscr():
        i = st["scr"]
        st["scr"] = (i + 1) % NSCR
        return scr[i]

    def nscr_bf():
        i = st["scr"]
        st["scr"] = (i + 1) % NSCR
        return scr_bf[i]

    def P(b):
        return ("ps", b)

    S.dma("sp", cp.t[:, :], cp_d[:, :], "c0", writes=[cp.reg()])
    S.dma("sp", ident.t[:, :], mats_d[0, :, :], "c1", writes=[ident.reg()])
    S.dma("sp", ones.t[:, :], mats_d[1, :, :], "c2", writes=[ones.reg()])
    S.dma("pool", ident_bf.t[:, :], mats_d[0, :, :], "c4", writes=[ident_bf.reg()])
    S.dma("pool", ones_bf.t[:, :], mats_d[1, :, :], "c5", writes=[ones_bf.reg()])
    for i in range(4):
        S.dma("sp", gb.t[:, i, :], gb_d[i, :, :], "c3_%d" % i, writes=[gb.reg(i)])

    wslot_ctr = [0]

    def load_w(idx, nel=SLOT):
        k = st["w"]
        st["w"] = (k + 1) % NW
        wb = wring[k]
        S.dma("pool", wb.t[:, 0:nel], wq_d[idx, :, 0:nel], "w%d" % k, writes=[wb.reg()])
        return wb

    def cpc(col):
        return cp.t[:, col:col + 1]

    def stage0(x_src, nrow, dst, dst_cols):
        k = st["xr"]
        st["xr"] = (k + 1) % 4
        xb = xring[k]
        S.dma("pool", xb.t[0:nrow, :], x_src, "xr%d" % k, writes=[xb.reg()])
        for g in range(4):
            b = naux()

            def fn(e, g=g, b=b, xb=xb):
                ins = None
                for q in range(4):
                    kc = 4 * g + q
                    ins = e.transpose(psb[b][:, q * nrow:(q + 1) * nrow],
                                      xb.t[0:nrow, kc * 128:(kc + 1) * 128], ident_bf.t[0:nrow, 0:nrow])
                return ins
            S.op("pe", fn, reads=[xb.reg(), ident_bf.reg()], writes=[P(b)])
            src = psb[b][:, 0:4 * nrow].rearrange("p (q t) -> p q t", q=4)
            S.op("act", lambda e, src=src, g=g: e.activation(dst.t[:, 4 * g:4 * g + 4, dst_cols[0]:dst_cols[1]], src, AF.Copy),
                 reads=[P(b)], writes=[dst.reg(4 * g, 4 * g + 4)])

    def mm_group(e, b, n, pairs):
        ins = None
        L = len(pairs)
        for i, (lhsT, rhs) in enumerate(pairs):
            ins = e.matmul(ps[b][:, 0:n], lhsT, rhs, start=(i == 0), stop=(i == L - 1))
        return ins

    def stage1(xsrc, n, halo_pass, hooks=()):
        for m in range(4):
            wb = load_w(8 + m)
            for h in range(2):
                c = 2 * m + h
                b = nb()
                pairs = [(wb.t[:, kc * 256 + h * 128: kc * 256 + h * 128 + 128], xsrc.t[:, kc, 0:n]) for kc in range(16)]
                S.op("pe", lambda e, b=b, pairs=pairs: mm_group(e, b, n, pairs),
                     reads=[wb.reg(), xsrc.reg()], writes=[P(b)])
                if halo_pass:
                    dst, dreg = halo_p.t[:, c, :], halo_p.reg(c)
                else:
                    dst, dreg = pool_u.t[:, c, HALO:HALO + n], pool_u.reg(c, None, HALO, HALO + n)
                S.op("act", lambda e, dst=dst, b=b: e.activation(dst, ps[b][:, 0:n], AF.Copy),
                     reads=[P(b)], writes=[dreg])
        for j in range(8):
            wb = load_w(j)
            A, G = nb(), nb()
            for h, b in ((0, A), (1, G)):
                pairs = [(wb.t[:, kc * 256 + h * 128: kc * 256 + h * 128 + 128], xsrc.t[:, kc, 0:n]) for kc in range(16)]
                S.op("pe", lambda e, b=b, pairs=pairs: mm_group(e, b, n, pairs),
                     reads=[wb.reg(), xsrc.reg()], writes=[P(b)])
            sg = nscr()
            S.op("act", lambda e, sg=sg, G=G: e.activation(sg.t[:, 0:n], ps[G][:, 0:n], AF.Sigmoid),
                 reads=[P(G)], writes=[sg.reg()])
            if halo_pass:
                dst, dreg = halo_u.t[:, j, :], halo_u.reg(j)
            else:
                dst, dreg = u_bf.t[:, j, HALO:HALO + n], u_bf.reg(j, None, HALO, HALO + n)
            S.op("dve", lambda e, dst=dst, A=A, sg=sg: e.tensor_tensor(dst, ps[A][:, 0:n], sg.t[:, 0:n], ALU.mult),
                 reads=[P(A), sg.reg()], writes=[dreg])
            if hooks and j % 2 == 1:
                hooks[j // 2]()

    def pool_windows(first_tile, c0=0, c1=8):
        for c in range(c0, c1):
            g = c // 2
            w = 2 ** (g + 1)
            cur, cur_reg = (lambda lo, hi, c=c: pool_u.t[:, c, lo:hi]), pool_u.reg(c)
            lo = 0
            for l in range(1, g + 2):
                step = 2 ** (l - 1)
                nlo = lo + step
                ob = pscr[(l - 1) % 2]
                S.op("dve", lambda e, ob=ob, cur=cur, nlo=nlo, step=step:
                     e.tensor_tensor(ob.t[:, nlo:HALO + T], cur(nlo, HALO + T), cur(nlo - step, HALO + T - step), ALU.add),
                     reads=[cur_reg], writes=[ob.reg()])
                cur, cur_reg = (lambda lo_, hi_, ob=ob: ob.t[:, lo_:hi_]), ob.reg()
                lo = nlo
            if first_tile:
                S.op("dve", lambda e, cur=cur, g=g: e.tensor_tensor(cur(HALO, HALO + 16), cur(HALO, HALO + 16),
                                                                   cp.t[:, CP_CORR + 16 * g: CP_CORR + 16 * g + 16], ALU.mult),
                     reads=[cur_reg, cp.reg()], writes=[cur_reg])
            S.op("dve", lambda e, cur=cur, c=c, w=w:
                 e.scalar_tensor_tensor(pooled_bf.t[:, c, :], cur(HALO, HALO + T), 1.0 / w, pool_u.t[:, c, HALO:HALO + T],
                                        ALU.mult, ALU.subtract),
                 reads=[cur_reg, pool_u.reg(c)], writes=[pooled_bf.reg(c)])

    def stage2(first_tile):
        pend = []

        def stats_mm(j, hv, hs):
            for (bank, h) in ((6, hv), (7, hs)):
                def fn(e, bank=bank, h=h, j=j):
                    e.matmul(ps[bank][:, :], ones_bf.t[:, :], h.t[:, 0, :], start=(j == 0), stop=False)
                    return e.matmul(ps[bank][:, :], ones_bf.t[:, :], h.t[:, 1, :], start=False, stop=(j == 7))
                S.op("pe", fn, reads=[ones_bf.reg(), h.reg()], writes=[P(bank)])

        for j in range(8):
            wb = load_w(12 + j, KT * 128)
            V = nb()
            pairs = [(wb.t[:, k * 128:(k + 1) * 128], u_bf.t[:, j, 2 + k: 2 + k + T]) for k in range(KT)]
            S.op("pe", lambda e, V=V, pairs=pairs: mm_group(e, V, T, pairs), reads=[wb.reg(), u_bf.reg(j)], writes=[P(V)])
            S.op("act", lambda e, V=V, j=j: e.activation(v_sb.t[:, j, :], ps[V][:, :], AF.Identity, bias=cpc(CP_DWB + j)),
                 reads=[P(V), cp.reg()], writes=[v_sb.reg(j)])
            sq = nscr()
            S.op("act", lambda e, V=V, j=j, sq=sq: e.activation(sq.t[:, :], ps[V][:, :], AF.Square, bias=cpc(CP_DWB + j)),
                 reads=[P(V), cp.reg()], writes=[sq.reg()])
            hv, hs = nscr_bf(), nscr_bf()
            for (dst, srcap, srcreg) in ((hv, v_sb.t[:, j, :], v_sb.reg(j)), (hs, sq.t[:, :], sq.reg())):
                S.op("dve", lambda e, dst=dst, srcap=srcap: e.tensor_copy(dst.t[:, 0, :], srcap), reads=[srcreg], writes=[dst.reg(0)])
                S.op("dve", lambda e, dst=dst, srcap=srcap: e.tensor_tensor(dst.t[:, 1, :], srcap, dst.t[:, 0, :], ALU.subtract),
                     reads=[srcreg, dst.reg(0)], writes=[dst.reg(1)])
            pend.append((j, hv, hs))
            if len(pend) > 1:
                stats_mm(*pend.pop(0))
            if j == 1:
                pool_windows(first_tile, 0, 4)
            if j == 4:
                pool_windows(first_tile, 4, 8)
        while pend:
            stats_mm(*pend.pop(0))
        S.op("dve", lambda e: e.tensor_copy(mean_t.t[:, :], ps[6][:, :]), reads=[P(6)], writes=[mean_t.reg()])
        msq = nscr()
        S.op("dve", lambda e, msq=msq: e.tensor_tensor(msq.t[:, :], mean_t.t[:, :], mean_t.t[:, :], ALU.mult),
             reads=[mean_t.reg()], writes=[msq.reg()])
        S.op("dve", lambda e, msq=msq: e.tensor_tensor(rstd_t.t[:, :], ps[7][:, :], msq.t[:, :], ALU.subtract),
             reads=[P(7), msq.reg()], writes=[rstd_t.reg()])
        S.op("act", lambda e: e.activation(rstd_t.t[:, :], rstd_t.t[:, :], AF.Sqrt, bias=cpc(CP_EPS)),
             reads=[rstd_t.reg(), cp.reg()], writes=[rstd_t.reg()])
        S.op("dve", lambda e: e.reciprocal(rstd_t.t[:, :], rstd_t.t[:, :]),
             reads=[rstd_t.reg()], writes=[rstd_t.reg()])
        for j in range(8):
            y = nscr()
            S.op("dve", lambda e, y=y, j=j: e.tensor_tensor(y.t[:, :], v_sb.t[:, j, :], mean_t.t[:, :], ALU.subtract),
                 reads=[v_sb.reg(j), mean_t.reg()], writes=[y.reg()])
            S.op("dve", lambda e, y=y: e.tensor_tensor(y.t[:, :], y.t[:, :], rstd_t.t[:, :], ALU.mult),
                 reads=[y.reg(), rstd_t.reg()], writes=[y.reg()])
            S.op("act", lambda e, y=y, j=j: e.activation(cv_bf.t[:, j, :], y.t[:, :], AF.Silu,
                                                        bias=cpc(CP_LNB + j), scale=cpc(CP_LNG + j)),
                 reads=[y.reg(), cp.reg()], writes=[cv_bf.reg(j)])

    def stage3():
        LAG = 2
        live = {}

        def front(j):
            gw = load_w(21 + 2 * j)
            kcv = st["cv"]
            st["cv"] = (kcv + 1) % NCV
            cvp = cvring[kcv]
            S.dma("pool", cvp.t[:, :], wq_d[20 + 2 * j, :, 0:1280], "cv%d" % kcv, writes=[cvp.reg()])
            GC, GP, YP = nb(), nb(), nb()
            for h, b in ((0, GC), (1, GP)):
                pairs = [(gw.t[:, kc * 256 + h * 128: kc * 256 + h * 128 + 128], xT.t[:, kc, :]) for kc in range(16)]
                S.op("pe", lambda e, b=b, pairs=pairs: mm_group(e, b, T, pairs), reads=[gw.reg(), xT.reg()], writes=[P(b)])
            g = j // 4
            pairs = [(cvp.t[:, 1024 + k2 * 128: 1024 + k2 * 128 + 128], pooled_bf.t[:, 2 * g + k2, :]) for k2 in range(2)]
            S.op("pe", lambda e, b=YP, pairs=pairs: mm_group(e, b, T, pairs),
                 reads=[cvp.reg(), pooled_bf.reg(2 * g, 2 * g + 2)], writes=[P(YP)])
            sc, sp_ = nscr(), nscr()
            S.op("act", lambda e, sc=sc, GC=GC: e.activation(sc.t[:, :], ps[GC][:, :], AF.Sigmoid), reads=[P(GC)], writes=[sc.reg()])
            S.op("act", lambda e, sp_=sp_, GP=GP: e.activation(sp_.t[:, :], ps[GP][:, :], AF.Sigmoid), reads=[P(GP)], writes=[sp_.reg()])
            S.op("dve", lambda e, sp_=sp_, YP=YP, j=j: e.scalar_tensor_tensor(sp_.t[:, :], ps[YP][:, :], cpc(CP_PSC + j), sp_.t[:, :],
                                                                             ALU.mult, ALU.mult),
                 reads=[P(YP), sp_.reg(), cp.reg()], writes=[sp_.reg()])
            live[j] = (cvp, sc, sp_)

        def back(j):
            cvp, sc, sp_ = live.pop(j)
            YC = nb()
            pairs = [(cvp.t[:, kc * 128: kc * 128 + 128], cv_bf.t[:, kc, :]) for kc in range(8)]
            S.op("pe", lambda e, b=YC, pairs=pairs: mm_group(e, b, T, pairs), reads=[cvp.reg(), cv_bf.reg()], writes=[P(YC)])
            S.op("dve", lambda e, sc=sc, YC=YC: e.tensor_tensor(sc.t[:, :], ps[YC][:, :], sc.t[:, :], ALU.mult),
                 reads=[P(YC), sc.reg()], writes=[sc.reg()])
            S.op("dve", lambda e, sc=sc, sp_=sp_, j=j: e.tensor_tensor(merged.t[:, j, :], sc.t[:, :], sp_.t[:, :], ALU.add),
                 reads=[sc.reg(), sp_.reg()], writes=[merged.reg(j)])

        for j in range(16 + LAG):
            if j < 16:
                front(j)
            if j - LAG >= 0:
                back(j - LAG)

    def ln_stats(s, q):
        S.op("dve", lambda e: e.bn_stats(bnst.t[:, s, q * 6:(q + 1) * 6], tm.t[:, s, q * 512:(q + 1) * 512]),
             reads=[tm.reg(s, None, q * 512, (q + 1) * 512)], writes=[bnst.reg(s, None, q * 6, (q + 1) * 6)])

    def ln_norm(s, par, zdst=None):
        S.op("dve", lambda e: e.bn_aggr(mv.t[:, par, :], bnst.t[:, s, :].rearrange("p (q f) -> p q f", q=4)),
             reads=[bnst.reg(s)], writes=[mv.reg(par)])
        S.op("act", lambda e: e.activation(rs.t[:, par, 0:1], mv.t[:, par, 1:2], AF.Sqrt, bias=cpc(CP_EPS)),
             reads=[mv.reg(par), cp.reg()], writes=[rs.reg(par)])
        S.op("dve", lambda e: e.reciprocal(rs.t[:, par, 0:1], rs.t[:, par, 0:1]),
             reads=[rs.reg(par)], writes=[rs.reg(par)])
        S.op("dve", lambda e: e.scalar_tensor_tensor(rs.t[:, par, 1:2], mv.t[:, par, 0:1], -1.0, rs.t[:, par, 0:1], ALU.mult, ALU.mult),
             reads=[mv.reg(par), rs.reg(par)], writes=[rs.reg(par)])
        if zdst is None:
            S.op("act", lambda e: e.activation(tm.t[:, s, :], tm.t[:, s, :], AF.Identity, bias=rs.t[:, par, 1:2], scale=rs.t[:, par, 0:1]),
                 reads=[tm.reg(s), rs.reg(par)], writes=[tm.reg(s)])
        else:
            S.op("act", lambda e: e.activation(zdst.t[:, :], tm.t[:, s, :], AF.Identity, bias=rs.t[:, par, 1:2], scale=rs.t[:, par, 0:1]),
                 reads=[tm.reg(s), rs.reg(par)], writes=[zdst.reg()])

    def ln_affine(s, gi):
        S.op("dve", lambda e: e.tensor_tensor(tm.t[:, s, :], tm.t[:, s, :], gb.t[:, gi, :], ALU.mult),
             reads=[tm.reg(s), gb.reg(gi)], writes=[tm.reg(s)])
        S.op("dve", lambda e: e.tensor_tensor(tm.t[:, s, :], tm.t[:, s, :], gb.t[:, gi + 1, :], ALU.add),
             reads=[tm.reg(s), gb.reg(gi + 1)], writes=[tm.reg(s)])

    def layernorm(s, gi, par):
        ln_norm(s, par)
        ln_affine(s, gi)

    def tokmajor_matmul(src, nk, slot0, nkg):
        for n in range(4):
            B = [nb() for _ in range(4)]
            for kg in range(nkg):
                k0 = kg * 8
                k1 = min(nk, k0 + 8)
                wb = load_w(slot0 + n * nkg + kg, (k1 - k0) * 512)

                def fn(e, wb=wb, k0=k0, k1=k1, B=B):
                    ins = None
                    for kc in range(k0, k1):
                        for s in range(4):
                            ins = e.matmul(ps[B[s]][:, :], src.t[:, kc, s * 128:(s + 1) * 128],
                                           wb.t[:, (kc - k0) * 512:(kc - k0 + 1) * 512], start=(kc == 0), stop=(kc == nk - 1))
                    return ins
                S.op("pe", fn, reads=[wb.reg(), src.reg(k0, k1)], writes=[P(b) for b in B])
            for s in range(4):
                S.op("dve", lambda e, s=s, n=n, B=B: e.scalar_tensor_tensor(tm.t[:, s, n * 512:(n + 1) * 512], tm.t[:, s, n * 512:(n + 1) * 512],
                                                                           ALPHA, ps[B[s]][:, :], ALU.mult, ALU.add),
                     reads=[P(B[s]), tm.reg(s, None, n * 512, (n + 1) * 512)], writes=[tm.reg(s, None, n * 512, (n + 1) * 512)])
                ln_stats(s, n)

    out_toks = []

    def do_stage0(ti):
        for s in range(4):
            stage0(x_d[ti * T + s * 128: ti * T + (s + 1) * 128, :], 128, xT, (s * 128, (s + 1) * 128))

    def tile_body(ti, hooks):
        tok0 = ti * T
        S.op("dve", lambda e: e.tensor_copy(u_bf.t[:, :, 0:HALO], halo_u.t[:, :, :]), reads=[halo_u.reg()], writes=[u_bf.reg()])
        S.op("dve", lambda e: e.tensor_copy(pool_u.t[:, :, 0:HALO], halo_p.t[:, :, :]), reads=[halo_p.reg()], writes=[pool_u.reg()])
        if KSTOP < 1:
            return
        stage1(xT, T, False, hooks)
        S.op("dve", lambda e: e.tensor_copy(halo_u.t[:, :, :], u_bf.t[:, :, T:T + HALO]), reads=[u_bf.reg()], writes=[halo_u.reg()])
        S.op("dve", lambda e: e.tensor_copy(halo_p.t[:, :, :], pool_u.t[:, :, T:T + HALO]), reads=[pool_u.reg()], writes=[halo_p.reg()])
        if KSTOP < 2:
            return
        stage2(ti == 0)
        if KSTOP < 3:
            return
        stage3()
        if KSTOP < 4:
            return
        for s in range(4):
            S.dma("sp", tm.t[:, s, :], x_d[tok0 + s * 128: tok0 + (s + 1) * 128, :], "tm%d" % s, writes=[tm.reg(s)])
        tokmajor_matmul(merged, 16, 52, 2)
        for s in range(4):
            zb = xring[s]
            ln_norm(s, s, zdst=zb)
            for g in range(4):
                b = naux()

                def fn(e, g=g, b=b, zb=zb):
                    ins = None
                    for q in range(4):
                        kc = 4 * g + q
                        ins = e.transpose(psb[b][:, q * 128:(q + 1) * 128], zb.t[:, kc * 128:(kc + 1) * 128], ident_bf.t[:, :])
                    return ins
                S.op("pe", fn, reads=[zb.reg(), ident_bf.reg()], writes=[P(b)])
                for q in range(4):
                    kc = 4 * g + q
                    if g % 2 == 0:
                        S.op("act", lambda e, b=b, q=q, kc=kc, s=s: e.activation(hT.t[:, kc, s * 128:(s + 1) * 128], psb[b][:, q * 128:(q + 1) * 128],
                                                                               AF.Identity, bias=cpc(CP_B1 + kc), scale=cpc(CP_G1 + kc)),
                             reads=[P(b), cp.reg()], writes=[hT.reg(kc, None, s * 128, (s + 1) * 128)])
                    else:
                        S.op("dve", lambda e, b=b, q=q, kc=kc, s=s: e.tensor_scalar(hT.t[:, kc, s * 128:(s + 1) * 128], psb[b][:, q * 128:(q + 1) * 128],
                                                                                  cpc(CP_G1 + kc), cpc(CP_B1 + kc), ALU.mult, ALU.add),
                             reads=[P(b), cp.reg()], writes=[hT.reg(kc, None, s * 128, (s + 1) * 128)])
        for s in range(4):
            S.op("dve", lambda e, s=s: e.tensor_scalar(tm.t[:, s, :], tm.t[:, s, :], rs.t[:, s, 0:1], rs.t[:, s, 1:2], ALU.mult, ALU.add),
                 reads=[tm.reg(s), rs.reg(s)], writes=[tm.reg(s)])
            ln_affine(s, 0)
        if KSTOP < 6:
            return
        for j in range(NJ):
            wb = load_w(60 + j)
            G, U = nb(), nb()
            for h, b in ((0, G), (1, U)):
                pairs = [(wb.t[:, kc * 256 + h * 128: kc * 256 + h * 128 + 128], hT.t[:, kc, :]) for kc in range(16)]
                S.op("pe", lambda e, b=b, pairs=pairs: mm_group(e, b, T, pairs), reads=[wb.reg(), hT.reg()], writes=[P(b)])
            sg = nscr()
            S.op("act", lambda e, sg=sg, G=G: e.activation(sg.t[:, :], ps[G][:, :], AF.Silu), reads=[P(G)], writes=[sg.reg()])
            S.op("dve", lambda e, sg=sg, U=U, j=j: e.tensor_tensor(act.t[:, j, :], ps[U][:, :], sg.t[:, :], ALU.mult),
                 reads=[P(U), sg.reg()], writes=[act.reg(j)])
        if KSTOP < 7:
            return
        tokmajor_matmul(act, NJ, 104, 6)
        if ti + 1 < n_tiles:
            do_stage0(ti + 1)

    def ln2_hooks(ti):
        def mk(s):
            def h():
                ln_norm(s, s)
                ln_affine(s, 2)
                tok = S.dma("sp", out_d[ti * T + s * 128: ti * T + (s + 1) * 128, :], tm.t[:, s, :], "out%d" % s, reads=[tm.reg(s)])
                out_toks.append(tok)
            return h
        return [mk(s) for s in range(4)]

    stage0(xh_d[:, :], HALO, xTh, (0, HALO))
    stage1(xTh, HALO, True)
    do_stage0(0)
    hooks = ()
    for ti in range(n_tiles):
        tile_body(ti, hooks)
        hooks = ln2_hooks(ti)
    for h in hooks:
        h()
    for tok in out_toks[-4:]:
        S.wait_tok("sp", tok)
    if not out_toks:
        tok = S.dma("sp", out_d[0:128, :], gb.t[:, 0, :], "out0", reads=[gb.reg(0)])
        S.wait_tok("sp", tok)
    with nc.Block() as block:
        @block.tensor
        def _(e):
            for f in S.streams["pe"]:
                f(e)

        @block.scalar
        def _(e):
            for f in S.streams["act"]:
                f(e)

        @block.vector
        def _(e):
            for f in S.streams["dve"]:
                f(e)

        @block.gpsimd
        def _(e):
            for f in S.streams["pool"]:
                f(e)

        @block.sync
        def _(e):
            for f in S.streams["sp"]:
                f(e)
    return nc


def pack_weights(w_in, dw_kernel, w_conv_out, w_pool, w_out, w_ffn_in, w_ffn_out):
    wq = np.zeros((NSLOT, 128, SLOT), np.float32)

    def pair16(W, cols):
        sub = W[:, cols]
        return sub.reshape(16, 128, 256).transpose(1, 0, 2).reshape(128, 4096)

    ar = np.arange(128)
    i = 0
    for j in range(8):
        wq[i] = pair16(w_in, np.concatenate([j * 128 + ar, 1024 + j * 128 + ar])); i += 1
    for m in range(4):
        wq[i] = pair16(w_in, 2048 + m * 256 + np.arange(256)); i += 1
    for j in range(8):
        blk = np.zeros((128, KT, 128), np.float32)
        blk[ar, :, ar] = dw_kernel[:, j * 128:(j + 1) * 128].T
        wq[i, :, :KT * 128] = blk.reshape(128, KT * 128); i += 1
    for j in range(16):
        cvo = w_conv_out[:, j * 128:(j + 1) * 128].reshape(8, 128, 128).transpose(1, 0, 2).reshape(128, 1024)
        g = j // 4
        pw = w_pool[g][:, (j % 4) * 128:(j % 4 + 1) * 128].reshape(2, 128, 128).transpose(1, 0, 2).reshape(128, 256)
        wq[i, :, 0:1024] = cvo
        wq[i, :, 1024:1280] = pw
        i += 1
        wq[i] = pair16(w_in, np.concatenate([3072 + j * 128 + ar, 5120 + j * 128 + ar])); i += 1
    for n in range(4):
        for kg in range(2):
            blk = w_out[kg * 1024:(kg + 1) * 1024, n * 512:(n + 1) * 512].reshape(8, 128, 512).transpose(1, 0, 2)
            wq[i] = blk.reshape(128, 4096); i += 1
    for j in range(NJ):
        wq[i] = pair16(w_ffn_in, np.concatenate([j * 128 + ar, FH + j * 128 + ar])); i += 1
    for n in range(4):
        for kg in range(6):
            k0 = kg * 8
            k1 = min(NJ, k0 + 8)
            blk = w_ffn_out[k0 * 128:k1 * 128, n * 512:(n + 1) * 512].reshape(k1 - k0, 128, 512).transpose(1, 0, 2)
            wq[i, :, :(k1 - k0) * 512] = blk.reshape(128, (k1 - k0) * 512); i += 1
    assert i == NSLOT
    return wq


def pack_consts(dw_bias, conv_ln_g, conv_ln_b, pool_scale, ln1_g, ln1_b):
    cp = np.ones((128, NCP), np.float32)
    cp[:, CP_DWB:CP_DWB + 8] = dw_bias.reshape(8, 128).T
    cp[:, CP_LNG:CP_LNG + 8] = conv_ln_g.reshape(8, 128).T
    cp[:, CP_LNB:CP_LNB + 8] = conv_ln_b.reshape(8, 128).T
    cp[:, CP_PSC:CP_PSC + 16] = pool_scale.reshape(16, 128).T
    cp[:, CP_EPS] = EPS
    cp[:, CP_G1:CP_G1 + 16] = ln1_g.reshape(16, 128).T
    cp[:, CP_B1:CP_B1 + 16] = ln1_b.reshape(16, 128).T
    return cp


def start_corr():
    c = np.ones((128, 4 * 16), np.float32)
    for g in range(4):
        w = 2 ** (g + 1)
        for t in range(16):
            c[:, g * 16 + t] = w / min(t + 1, w)
    return c


_PROGRAMS = {}


def run_cores(x_rows_list, halo_list, start_flags, wq, cp, gbb, n_tiles):
    if n_tiles not in _PROGRAMS:
        _PROGRAMS[n_tiles] = build_program(n_tiles)
    nc = _PROGRAMS[n_tiles]
    mats = np.stack([np.eye(128, dtype=np.float32), np.full((128, 128), 1.0 / CW, np.float32)])
    corr = start_corr()
    in_maps = []
    for xr, xh, sf in zip(x_rows_list, halo_list, start_flags):
        cpc = cp.copy()
        if sf:
            cpc[:, CP_CORR:CP_CORR + 64] = corr
        in_maps.append({"x": np.ascontiguousarray(xr), "xh": np.ascontiguousarray(xh), "wq": wq, "cp": cpc, "gb": gbb, "mats": mats})
    res = run_bass_kernel_spmd(nc, in_maps, core_ids=list(range(len(in_maps))))
    return [r["out"] for r in res.results]


def kernel(x, w_in, dw_kernel, dw_bias, conv_ln_g, conv_ln_b, w_conv_out, w_pool, pool_scale, w_out,
           ln1_g, ln1_b, w_ffn_in, w_ffn_out, ln2_g, ln2_b):
    x = np.asarray(x, np.float32)
    B, SEQ, _ = x.shape
    f = lambda a: np.asarray(a, np.float32)[0]
    wq = pack_weights(f(w_in), f(dw_kernel), f(w_conv_out), f(w_pool), f(w_out), f(w_ffn_in), f(w_ffn_out))
    cp = pack_consts(f(dw_bias), f(conv_ln_g), f(conv_ln_b), f(pool_scale), f(ln1_g), f(ln1_b))
    gbb = np.stack([np.broadcast_to(f(a)[None, :], (128, D)) for a in (ln1_g, ln1_b, ln2_g, ln2_b)]).astype(np.float32)
    per_core = (B * SEQ) // NCORES
    n_tiles = per_core // T
    xs, hs, sf = [], [], []
    xf = x.reshape(B * SEQ, D)
    for c in range(NCORES):
        r0 = c * per_core
        xs.append(xf[r0:r0 + per_core])
        if r0 % SEQ == 0:
            hs.append(np.zeros((HALO, D), np.float32)); sf.append(True)
        else:
            hs.append(xf[r0 - HALO:r0]); sf.append(False)
    outs = run_cores(xs, hs, sf, wq, cp, gbb, n_tiles)
    return np.concatenate(outs, axis=0).reshape(B, SEQ, D).astype(np.float32)
```

```python
import math
import os
KSTOP = int(os.environ.get('KSTOP', '99'))
import numpy as np
import concourse.bass as bass
import concourse.mybir as mybir
from concourse.bass_utils import run_bass_kernel_spmd

F32 = mybir.dt.float32
BF16 = mybir.dt.bfloat16
AF = mybir.ActivationFunctionType
ALU = mybir.AluOpType

D = 2048
T = 512
NCORES = 8
CW = 1024
KT = 31
FH = 5632
NJ = FH // 128
HALO = 32
ALPHA = 2.0 ** 0.25
EPS = 1e-5
SLOT = 4096
NSLOT = 8 + 4 + 8 + 16 + 16 + 8 + 44 + 24
CP_DWB, CP_LNG, CP_LNB, CP_PSC, CP_CORR = 0, 8, 16, 24, 40
CP_EPS = 40 + 4 * 16
CP_G1 = CP_EPS + 1
CP_B1 = CP_G1 + 16
NCP = CP_B1 + 16


class SB:
    def __init__(self, nc, name, shape, dtype, off):
        self.t = nc.alloc_sbuf_tensor_at(name, list(shape), dtype, offset=off)
        self.off = off
        self.esz = 2 if dtype == BF16 else 4
        self.shape = list(shape)
        self.row = int(np.prod(shape[1:])) * self.esz
        self.chunk = (int(np.prod(shape[2:])) * self.esz) if len(shape) > 2 else self.row

    def reg(self, c0=None, c1=None, lo=None, hi=None):
        if c0 is None:
            return (self.off, self.off + self.row)
        if c1 is None:
            c1 = c0 + 1
        if lo is None:
            return (self.off + c0 * self.chunk, self.off + c1 * self.chunk)
        assert c1 == c0 + 1
        return (self.off + c0 * self.chunk + lo * self.esz, self.off + c0 * self.chunk + hi * self.esz)


class Sched:
    ENG = ["pe", "act", "dve", "pool", "sp"]

    def __init__(self, nc):
        self.nc = nc
        self.streams = {e: [] for e in self.ENG}
        self.seq = {e: 0 for e in self.ENG}
        self.semh = {e: nc.alloc_semaphore("sem_" + e) for e in ["pe", "act", "dve", "pool"]}
        self.known = {e: {} for e in self.ENG}
        self.w = {}
        self.r = {}
        self.dcnt = {}
        self.nwaits = 0

    def _atoms(self, regs):
        for rg in regs:
            if isinstance(rg[0], str):
                yield rg
            else:
                lo, hi = rg
                for a in range(lo >> 8, ((hi - 1) >> 8) + 1):
                    yield a

    def _deps(self, reads, writes):
        need = {}

        def add(k, v):
            if need.get(k, 0) < v:
                need[k] = v

        for a in self._atoms(reads):
            t = self.w.get(a)
            if t is not None:
                add(*t)
        for a in self._atoms(writes):
            t = self.w.get(a)
            if t is not None:
                add(*t)
            rr = self.r.get(a)
            if rr:
                for k, v in rr.items():
                    add(k, v)
        return need

    def _waits(self, eng, need):
        for k, v in need.items():
            if k == eng and eng == "pe":
                continue
            if self.known[eng].get(k, 0) >= v:
                continue
            self.known[eng][k] = v
            h = self.semh[k]
            self.nwaits += 1
            self.streams[eng].append(lambda e, h=h, v=v: e.wait_ge(h, v))

    def _update(self, tok, reads, writes):
        k, v = tok
        for a in self._atoms(writes):
            self.w[a] = tok
            self.r[a] = None
        for a in self._atoms(reads):
            rr = self.r.get(a)
            if rr is None:
                self.r[a] = {k: v}
            elif rr.get(k, 0) < v:
                rr[k] = v

    def op(self, eng, fn, reads=(), writes=()):
        reads = list(reads)
        writes = list(writes)
        self._waits(eng, self._deps(reads, writes))
        self.seq[eng] += 1
        v = self.seq[eng]
        h = self.semh[eng]
        self.streams[eng].append(lambda e, fn=fn, h=h: fn(e).then_inc(h, 1))
        self._update((eng, v), reads, writes)

    def dma(self, q, out, in_, semname, reads=(), writes=()):
        reads = list(reads)
        writes = list(writes)
        self._waits(q, self._deps(reads, writes))
        if semname not in self.semh:
            self.semh[semname] = self.nc.alloc_semaphore("sem_" + semname)
            self.dcnt[semname] = 0
        self.dcnt[semname] += 16
        v = self.dcnt[semname]
        h = self.semh[semname]
        self.streams[q].append(lambda e, out=out, in_=in_, h=h: e.dma_start(out=out, in_=in_).then_inc(h, 16))
        self._update((semname, v), reads, writes)
        return (semname, v)

    def wait_tok(self, eng, tok):
        self._waits(eng, {tok[0]: tok[1]})


def build_program(n_tiles):
    nc = bass.Bass("TRN2", target_bir_lowering=False)
    n_tok = n_tiles * T
    x_d = nc.dram_tensor("x", [n_tok, D], F32, kind="ExternalInput").ap()
    xh_d = nc.dram_tensor("xh", [HALO, D], F32, kind="ExternalInput").ap()
    wq_d = nc.dram_tensor("wq", [NSLOT, 128, SLOT], F32, kind="ExternalInput").ap()
    cp_d = nc.dram_tensor("cp", [128, NCP], F32, kind="ExternalInput").ap()
    gb_d = nc.dram_tensor("gb", [4, 128, D], F32, kind="ExternalInput").ap()
    mats_d = nc.dram_tensor("mats", [2, 128, 128], F32, kind="ExternalInput").ap()
    out_d = nc.dram_tensor("out", [n_tok, D], F32, kind="ExternalOutput").ap()

    S = Sched(nc)
    off = [((nc.sbuf_base + 255) // 256) * 256]

    def alloc(name, shape, dtype, at=None):
        if at is not None:
            return SB(nc, name, shape, dtype, at)
        b = SB(nc, name, shape, dtype, off[0])
        off[0] += ((b.row + 255) // 256) * 256
        return b

    cp = alloc("cp", [128, NCP], F32)
    ident = alloc("ident", [128, 128], F32)
    ident_bf = alloc("ident_bf", [128, 128], BF16)
    ones_bf = alloc("ones_bf", [128, 128], BF16)
    ones = alloc("ones", [128, 128], F32)
    halo_u = alloc("halo_u", [128, 8, HALO], BF16)
    halo_p = alloc("halo_p", [128, 8, HALO], F32)
    xTh = alloc("xTh", [128, 16, HALO], BF16)
    bnst = alloc("bnst", [128, 4, 4 * 6], F32)
    mv = alloc("mv", [128, 4, 2], F32)
    rs = alloc("rs", [128, 4, 2], F32)
    gb = alloc("gb", [128, 4, D], F32)
    xT = alloc("xT", [128, 16, T], BF16)
    mh_off = off[0]
    merged = alloc("merged", [128, 16, T], BF16)
    hT = alloc("hT", [128, 16, T], BF16, at=mh_off)
    acta_off = off[0]
    act = alloc("act", [128, NJ, T], BF16)
    pool_u = alloc("pool_u", [128, 8, HALO + T], F32, at=acta_off)
    pscr = [alloc("pscr%d" % i, [128, HALO + T], F32, at=acta_off + 17408 + i * 2176) for i in range(2)]
    NCV = 4
    cvring = [alloc("cvring%d" % i, [128, 1280], BF16, at=acta_off + i * 2560) for i in range(NCV)]
    cv_bf = alloc("cv_bf", [128, 8, T], BF16, at=acta_off + 21760)
    pooled_bf = alloc("pooled_bf", [128, 8, T], BF16, at=acta_off + 29952)
    tm_off = off[0]
    tm = alloc("tm", [128, 4, D], F32)
    v_sb = alloc("v_sb", [128, 8, T], F32, at=mh_off)
    xring = [alloc("xring%d" % i, [128, D], BF16) for i in range(4)]
    u_bf = alloc("u_bf", [128, 8, HALO + T], BF16, at=xring[0].off)
    NW = 4
    wring = [alloc("wring%d" % i, [128, SLOT], BF16) for i in range(NW)]
    mean_t = alloc("mean_t", [128, T], F32, at=acta_off + 38144)
    rstd_t = alloc("rstd_t", [128, T], F32, at=acta_off + 38144 + 2048)
    NSCR = 6
    scr = [alloc("scr%d" % i, [128, T], F32) for i in range(NSCR)]
    scr_bf = [alloc("scrbf%d" % i, [128, 2, T], BF16, at=scr[i].off) for i in range(NSCR)]
    assert off[0] <= nc.sbuf_top, (off[0], nc.sbuf_top)

    ps = [nc.alloc_psum_tensor("ps%d" % i, [128, 512], F32) for i in range(8)]
    psb = [p_.bitcast(BF16) for p_ in ps]
    st = {"bank": 0, "scr": 0, "w": 0, "aux": 0, "xr": 0, "cv": 0}

    def nb():
        b = st["bank"]
        st["bank"] = (b + 1) % 6
        return b

    def naux():
        b = 6 + st["aux"]
        st["aux"] ^= 1
        return b

    def nscr():
        i = st["scr"]
        st["scr"] = (i + 1) % NSCR
        return scr[i]

    def nscr_bf():
        i = st["scr"]
        st["scr"] = (i + 1) % NSCR
        return scr_bf[i]

    def P(b):
        return ("ps", b)

    S.dma("sp", cp.t[:, :], cp_d[:, :], "c0", writes=[cp.reg()])
    S.dma("sp", ident.t[:, :], mats_d[0, :, :], "c1", writes=[ident.reg()])
    S.dma("sp", ones.t[:, :], mats_d[1, :, :], "c2", writes=[ones.reg()])
    S.dma("pool", ident_bf.t[:, :], mats_d[0, :, :], "c4", writes=[ident_bf.reg()])
    S.dma("pool", ones_bf.t[:, :], mats_d[1, :, :], "c5", writes=[ones_bf.reg()])
    for i in range(4):
        S.dma("sp", gb.t[:, i, :], gb_d[i, :, :], "c3_%d" % i, writes=[gb.reg(i)])

    wslot_ctr = [0]

    def load_w(idx, nel=SLOT):
        k = st["w"]
        st["w"] = (k + 1) % NW
        wb = wring[k]
        S.dma("pool", wb.t[:, 0:nel], wq_d[idx, :, 0:nel], "w%d" % k, writes=[wb.reg()])
        return wb

    def cpc(col):
        return cp.t[:, col:col + 1]

    def stage0(x_src, nrow, dst, dst_cols):
        k = st["xr"]
        st["xr"] = (k + 1) % 4
        xb = xring[k]
        S.dma("pool", xb.t[0:nrow, :], x_src, "xr%d" % k, writes=[xb.reg()])
        for g in range(4):
            b = naux()

            def fn(e, g=g, b=b, xb=xb):
                ins = None
                for q in range(4):
                    kc = 4 * g + q
                    ins = e.transpose(psb[b][:, q * nrow:(q + 1) * nrow],
                                      xb.t[0:nrow, kc * 128:(kc + 1) * 128], ident_bf.t[0:nrow, 0:nrow])
                return ins
            S.op("pe", fn, reads=[xb.reg(), ident_bf.reg()], writes=[P(b)])
            src = psb[b][:, 0:4 * nrow].rearrange("p (q t) -> p q t", q=4)
            S.op("act", lambda e, src=src, g=g: e.activation(dst.t[:, 4 * g:4 * g + 4, dst_cols[0]:dst_cols[1]], src, AF.Copy),
                 reads=[P(b)], writes=[dst.reg(4 * g, 4 * g + 4)])

    def mm_group(e, b, n, pairs):
        ins = None
        L = len(pairs)
        for i, (lhsT, rhs) in enumerate(pairs):
            ins = e.matmul(ps[b][:, 0:n], lhsT, rhs, start=(i == 0), stop=(i == L - 1))
        return ins

    def stage1(xsrc, n, halo_pass, hooks=()):
        for m in range(4):
            wb = load_w(8 + m)
            for h in range(2):
                c = 2 * m + h
                b = nb()
                pairs = [(wb.t[:, kc * 256 + h * 128: kc * 256 + h * 128 + 128], xsrc.t[:, kc, 0:n]) for kc in range(16)]
                S.op("pe", lambda e, b=b, pairs=pairs: mm_group(e, b, n, pairs),
                     reads=[wb.reg(), xsrc.reg()], writes=[P(b)])
                if halo_pass:
                    dst, dreg = halo_p.t[:, c, :], halo_p.reg(c)
                else:
                    dst, dreg = pool_u.t[:, c, HALO:HALO + n], pool_u.reg(c, None, HALO, HALO + n)
                S.op("act", lambda e, dst=dst, b=b: e.activation(dst, ps[b][:, 0:n], AF.Copy),
                     reads=[P(b)], writes=[dreg])
        for j in range(8):
            wb = load_w(j)
            A, G = nb(), nb()
            for h, b in ((0, A), (1, G)):
                pairs = [(wb.t[:, kc * 256 + h * 128: kc * 256 + h * 128 + 128], xsrc.t[:, kc, 0:n]) for kc in range(16)]
                S.op("pe", lambda e, b=b, pairs=pairs: mm_group(e, b, n, pairs),
                     reads=[wb.reg(), xsrc.reg()], writes=[P(b)])
            sg = nscr()
            S.op("act", lambda e, sg=sg, G=G: e.activation(sg.t[:, 0:n], ps[G][:, 0:n], AF.Sigmoid),
                 reads=[P(G)], writes=[sg.reg()])
            if halo_pass:
                dst, dreg = halo_u.t[:, j, :], halo_u.reg(j)
            else:
                dst, dreg = u_bf.t[:, j, HALO:HALO + n], u_bf.reg(j, None, HALO, HALO + n)
            S.op("dve", lambda e, dst=dst, A=A, sg=sg: e.tensor_tensor(dst, ps[A][:, 0:n], sg.t[:, 0:n], ALU.mult),
                 reads=[P(A), sg.reg()], writes=[dreg])
            if hooks and j % 2 == 1:
                hooks[j // 2]()

    def pool_windows(first_tile, c0=0, c1=8):
        for c in range(c0, c1):
            g = c // 2
            w = 2 ** (g + 1)
            cur, cur_reg = (lambda lo, hi, c=c: pool_u.t[:, c, lo:hi]), pool_u.reg(c)
            lo = 0
            for l in range(1, g + 2):
                step = 2 ** (l - 1)
                nlo = lo + step
                ob = pscr[(l - 1) % 2]
                S.op("dve", lambda e, ob=ob, cur=cur, nlo=nlo, step=step:
                     e.tensor_tensor(ob.t[:, nlo:HALO + T], cur(nlo, HALO + T), cur(nlo - step, HALO + T - step), ALU.add),
                     reads=[cur_reg], writes=[ob.reg()])
                cur, cur_reg = (lambda lo_, hi_, ob=ob: ob.t[:, lo_:hi_]), ob.reg()
                lo = nlo
            if first_tile:
                S.op("dve", lambda e, cur=cur, g=g: e.tensor_tensor(cur(HALO, HALO + 16), cur(HALO, HALO + 16),
                                                                   cp.t[:, CP_CORR + 16 * g: CP_CORR + 16 * g + 16], ALU.mult),
                     reads=[cur_reg, cp.reg()], writes=[cur_reg])
            S.op("dve", lambda e, cur=cur, c=c, w=w:
                 e.scalar_tensor_tensor(pooled_bf.t[:, c, :], cur(HALO, HALO + T), 1.0 / w, pool_u.t[:, c, HALO:HALO + T],
                                        ALU.mult, ALU.subtract),
                 reads=[cur_reg, pool_u.reg(c)], writes=[pooled_bf.reg(c)])

    def stage2(first_tile):
        pend = []

        def stats_mm(j, hv, hs):
            for (bank, h) in ((6, hv), (7, hs)):
                def fn(e, bank=bank, h=h, j=j):
                    e.matmul(ps[bank][:, :], ones_bf.t[:, :], h.t[:, 0, :], start=(j == 0), stop=False)
                    return e.matmul(ps[bank][:, :], ones_bf.t[:, :], h.t[:, 1, :], start=False, stop=(j == 7))
                S.op("pe", fn, reads=[ones_bf.reg(), h.reg()], writes=[P(bank)])

        for j in range(8):
            wb = load_w(12 + j, KT * 128)
            V = nb()
            pairs = [(wb.t[:, k * 128:(k + 1) * 128], u_bf.t[:, j, 2 + k: 2 + k + T]) for k in range(KT)]
            S.op("pe", lambda e, V=V, pairs=pairs: mm_group(e, V, T, pairs), reads=[wb.reg(), u_bf.reg(j)], writes=[P(V)])
            S.op("act", lambda e, V=V, j=j: e.activation(v_sb.t[:, j, :], ps[V][:, :], AF.Identity, bias=cpc(CP_DWB + j)),
                 reads=[P(V), cp.reg()], writes=[v_sb.reg(j)])
            sq = nscr()
            S.op("act", lambda e, V=V, j=j, sq=sq: e.activation(sq.t[:, :], ps[V][:, :], AF.Square, bias=cpc(CP_DWB + j)),
                 reads=[P(V), cp.reg()], writes=[sq.reg()])
            hv, hs = nscr_bf(), nscr_bf()
            for (dst, srcap, srcreg) in ((hv, v_sb.t[:, j, :], v_sb.reg(j)), (hs, sq.t[:, :], sq.reg())):
                S.op("dve", lambda e, dst=dst, srcap=srcap: e.tensor_copy(dst.t[:, 0, :], srcap), reads=[srcreg], writes=[dst.reg(0)])
                S.op("dve", lambda e, dst=dst, srcap=srcap: e.tensor_tensor(dst.t[:, 1, :], srcap, dst.t[:, 0, :], ALU.subtract),
                     reads=[srcreg, dst.reg(0)], writes=[dst.reg(1)])
            pend.append((j, hv, hs))
            if len(pend) > 1:
                stats_mm(*pend.pop(0))
            if j == 1:
                pool_windows(first_tile, 0, 4)
            if j == 4:
                pool_windows(first_tile, 4, 8)
        while pend:
            stats_mm(*pend.pop(0))
        S.op("dve", lambda e: e.tensor_copy(mean_t.t[:, :], ps[6][:, :]), reads=[P(6)], writes=[mean_t.reg()])
        msq = nscr()
        S.op("dve", lambda e, msq=msq: e.tensor_tensor(msq.t[:, :], mean_t.t[:, :], mean_t.t[:, :], ALU.mult),
             reads=[mean_t.reg()], writes=[msq.reg()])
        S.op("dve", lambda e, msq=msq: e.tensor_tensor(rstd_t.t[:, :], ps[7][:, :], msq.t[:, :], ALU.subtract),
             reads=[P(7), msq.reg()], writes=[rstd_t.reg()])
        S.op("act", lambda e: e.activation(rstd_t.t[:, :], rstd_t.t[:, :], AF.Sqrt, bias=cpc(CP_EPS)),
             reads=[rstd_t.reg(), cp.reg()], writes=[rstd_t.reg()])
        S.op("dve", lambda e: e.reciprocal(rstd_t.t[:, :], rstd_t.t[:, :]),
             reads=[rstd_t.reg()], writes=[rstd_t.reg()])
        for j in range(8):
            y = nscr()
            S.op("dve", lambda e, y=y, j=j: e.tensor_tensor(y.t[:, :], v_sb.t[:, j, :], mean_t.t[:, :], ALU.subtract),
                 reads=[v_sb.reg(j), mean_t.reg()], writes=[y.reg()])
            S.op("dve", lambda e, y=y: e.tensor_tensor(y.t[:, :], y.t[:, :], rstd_t.t[:, :], ALU.mult),
                 reads=[y.reg(), rstd_t.reg()], writes=[y.reg()])
            S.op("act", lambda e, y=y, j=j: e.activation(cv_bf.t[:, j, :], y.t[:, :], AF.Silu,
                                                        bias=cpc(CP_LNB + j), scale=cpc(CP_LNG + j)),
                 reads=[y.reg(), cp.reg()], writes=[cv_bf.reg(j)])

    def stage3():
        LAG = 2
        live = {}

        def front(j):
            gw = load_w(21 + 2 * j)
            kcv = st["cv"]
            st["cv"] = (kcv + 1) % NCV
            cvp = cvring[kcv]
            S.dma("pool", cvp.t[:, :], wq_d[20 + 2 * j, :, 0:1280], "cv%d" % kcv, writes=[cvp.reg()])
            GC, GP, YP = nb(), nb(), nb()
            for h, b in ((0, GC), (1, GP)):
                pairs = [(gw.t[:, kc * 256 + h * 128: kc * 256 + h * 128 + 128], xT.t[:, kc, :]) for kc in range(16)]
                S.op("pe", lambda e, b=b, pairs=pairs: mm_group(e, b, T, pairs), reads=[gw.reg(), xT.reg()], writes=[P(b)])
            g = j // 4
            pairs = [(cvp.t[:, 1024 + k2 * 128: 1024 + k2 * 128 + 128], pooled_bf.t[:, 2 * g + k2, :]) for k2 in range(2)]
            S.op("pe", lambda e, b=YP, pairs=pairs: mm_group(e, b, T, pairs),
                 reads=[cvp.reg(), pooled_bf.reg(2 * g, 2 * g + 2)], writes=[P(YP)])
            sc, sp_ = nscr(), nscr()
            S.op("act", lambda e, sc=sc, GC=GC: e.activation(sc.t[:, :], ps[GC][:, :], AF.Sigmoid), reads=[P(GC)], writes=[sc.reg()])
            S.op("act", lambda e, sp_=sp_, GP=GP: e.activation(sp_.t[:, :], ps[GP][:, :], AF.Sigmoid), reads=[P(GP)], writes=[sp_.reg()])
            S.op("dve", lambda e, sp_=sp_, YP=YP, j=j: e.scalar_tensor_tensor(sp_.t[:, :], ps[YP][:, :], cpc(CP_PSC + j), sp_.t[:, :],
                                                                             ALU.mult, ALU.mult),
                 reads=[P(YP), sp_.reg(), cp.reg()], writes=[sp_.reg()])
            live[j] = (cvp, sc, sp_)

        def back(j):
            cvp, sc, sp_ = live.pop(j)
            YC = nb()
            pairs = [(cvp.t[:, kc * 128: kc * 128 + 128], cv_bf.t[:, kc, :]) for kc in range(8)]
            S.op("pe", lambda e, b=YC, pairs=pairs: mm_group(e, b, T, pairs), reads=[cvp.reg(), cv_bf.reg()], writes=[P(YC)])
            S.op("dve", lambda e, sc=sc, YC=YC: e.tensor_tensor(sc.t[:, :], ps[YC][:, :], sc.t[:, :], ALU.mult),
                 reads=[P(YC), sc.reg()], writes=[sc.reg()])
            S.op("dve", lambda e, sc=sc, sp_=sp_, j=j: e.tensor_tensor(merged.t[:, j, :], sc.t[:, :], sp_.t[:, :], ALU.add),
                 reads=[sc.reg(), sp_.reg()], writes=[merged.reg(j)])

        for j in range(16 + LAG):
            if j < 16:
                front(j)
            if j - LAG >= 0:
                back(j - LAG)

    def ln_stats(s, q):
        S.op("dve", lambda e: e.bn_stats(bnst.t[:, s, q * 6:(q + 1) * 6], tm.t[:, s, q * 512:(q + 1) * 512]),
             reads=[tm.reg(s, None, q * 512, (q + 1) * 512)], writes=[bnst.reg(s, None, q * 6, (q + 1) * 6)])

    def ln_norm(s, par, zdst=None):
        S.op("dve", lambda e: e.bn_aggr(mv.t[:, par, :], bnst.t[:, s, :].rearrange("p (q f) -> p q f", q=4)),
             reads=[bnst.reg(s)], writes=[mv.reg(par)])
        S.op("act", lambda e: e.activation(rs.t[:, par, 0:1], mv.t[:, par, 1:2], AF.Sqrt, bias=cpc(CP_EPS)),
             reads=[mv.reg(par), cp.reg()], writes=[rs.reg(par)])
        S.op("dve", lambda e: e.reciprocal(rs.t[:, par, 0:1], rs.t[:, par, 0:1]),
             reads=[rs.reg(par)], writes=[rs.reg(par)])
        S.op("dve", lambda e: e.scalar_tensor_tensor(rs.t[:, par, 1:2], mv.t[:, par, 0:1], -1.0, rs.t[:, par, 0:1], ALU.mult, ALU.mult),
             reads=[mv.reg(par), rs.reg(par)], writes=[rs.reg(par)])
        if zdst is None:
            S.op("act", lambda e: e.activation(tm.t[:, s, :], tm.t[:, s, :], AF.Identity, bias=rs.t[:, par, 1:2], scale=rs.t[:, par, 0:1]),
                 reads=[tm.reg(s), rs.reg(par)], writes=[tm.reg(s)])
        else:
            S.op("act", lambda e: e.activation(zdst.t[:, :], tm.t[:, s, :], AF.Identity, bias=rs.t[:, par, 1:2], scale=rs.t[:, par, 0:1]),
                 reads=[tm.reg(s), rs.reg(par)], writes=[zdst.reg()])

    def ln_affine(s, gi):
        S.op("dve", lambda e: e.tensor_tensor(tm.t[:, s, :], tm.t[:, s, :], gb.t[:, gi, :], ALU.mult),
             reads=[tm.reg(s), gb.reg(gi)], writes=[tm.reg(s)])
        S.op("dve", lambda e: e.tensor_tensor(tm.t[:, s, :], tm.t[:, s, :], gb.t[:, gi + 1, :], ALU.add),
             reads=[tm.reg(s), gb.reg(gi + 1)], writes=[tm.reg(s)])

    def layernorm(s, gi, par):
        ln_norm(s, par)
        ln_affine(s, gi)

    def tokmajor_matmul(src, nk, slot0, nkg):
        for n in range(4):
            B = [nb() for _ in range(4)]
            for kg in range(nkg):
                k0 = kg * 8
                k1 = min(nk, k0 + 8)
                wb = load_w(slot0 + n * nkg + kg, (k1 - k0) * 512)

                for half in ((0, 1), (2, 3)):
                    def fn(e, wb=wb, k0=k0, k1=k1, B=B, half=half):
                        ins = None
                        for kc in range(k0, k1):
                            for s in half:
                                ins = e.matmul(ps[B[s]][:, :], src.t[:, kc, s * 128:(s + 1) * 128],
                                               wb.t[:, (kc - k0) * 512:(kc - k0 + 1) * 512], start=(kc == 0), stop=(kc == nk - 1))
                        return ins
                    S.op("pe", fn, reads=[wb.reg(), src.reg(k0, k1)], writes=[P(B[s]) for s in half])
            for s in range(4):
                S.op("dve", lambda e, s=s, n=n, B=B: e.scalar_tensor_tensor(tm.t[:, s, n * 512:(n + 1) * 512], tm.t[:, s, n * 512:(n + 1) * 512],
                                                                           ALPHA, ps[B[s]][:, :], ALU.mult, ALU.add),
                     reads=[P(B[s]), tm.reg(s, None, n * 512, (n + 1) * 512)], writes=[tm.reg(s, None, n * 512, (n + 1) * 512)])
                ln_stats(s, n)

    out_toks = []

    def do_stage0(ti):
        for s in range(4):
            stage0(x_d[ti * T + s * 128: ti * T + (s + 1) * 128, :], 128, xT, (s * 128, (s + 1) * 128))

    def tile_body(ti, hooks):
        tok0 = ti * T
        S.op("dve", lambda e: e.tensor_copy(u_bf.t[:, :, 0:HALO], halo_u.t[:, :, :]), reads=[halo_u.reg()], writes=[u_bf.reg()])
        S.op("dve", lambda e: e.tensor_copy(pool_u.t[:, :, 0:HALO], halo_p.t[:, :, :]), reads=[halo_p.reg()], writes=[pool_u.reg()])
        if KSTOP < 1:
            return
        stage1(xT, T, False, hooks)
        S.op("dve", lambda e: e.tensor_copy(halo_u.t[:, :, :], u_bf.t[:, :, T:T + HALO]), reads=[u_bf.reg()], writes=[halo_u.reg()])
        S.op("dve", lambda e: e.tensor_copy(halo_p.t[:, :, :], pool_u.t[:, :, T:T + HALO]), reads=[pool_u.reg()], writes=[halo_p.reg()])
        if KSTOP < 2:
            return
        stage2(ti == 0)
        if KSTOP < 3:
            return
        stage3()
        if KSTOP < 4:
            return
        for s in range(4):
            S.dma("sp", tm.t[:, s, :], x_d[tok0 + s * 128: tok0 + (s + 1) * 128, :], "tm%d" % s, writes=[tm.reg(s)])
        tokmajor_matmul(merged, 16, 52, 2)
        for s in range(4):
            zb = xring[s]
            ln_norm(s, s, zdst=zb)
            for g in range(4):
                b = naux()

                def fn(e, g=g, b=b, zb=zb):
                    ins = None
                    for q in range(4):
                        kc = 4 * g + q
                        ins = e.transpose(psb[b][:, q * 128:(q + 1) * 128], zb.t[:, kc * 128:(kc + 1) * 128], ident_bf.t[:, :])
                    return ins
                S.op("pe", fn, reads=[zb.reg(), ident_bf.reg()], writes=[P(b)])
                for q in range(4):
                    kc = 4 * g + q
                    if g % 2 == 0:
                        S.op("act", lambda e, b=b, q=q, kc=kc, s=s: e.activation(hT.t[:, kc, s * 128:(s + 1) * 128], psb[b][:, q * 128:(q + 1) * 128],
                                                                               AF.Identity, bias=cpc(CP_B1 + kc), scale=cpc(CP_G1 + kc)),
                             reads=[P(b), cp.reg()], writes=[hT.reg(kc, None, s * 128, (s + 1) * 128)])
                    else:
                        S.op("dve", lambda e, b=b, q=q, kc=kc, s=s: e.tensor_scalar(hT.t[:, kc, s * 128:(s + 1) * 128], psb[b][:, q * 128:(q + 1) * 128],
                                                                                  cpc(CP_G1 + kc), cpc(CP_B1 + kc), ALU.mult, ALU.add),
                             reads=[P(b), cp.reg()], writes=[hT.reg(kc, None, s * 128, (s + 1) * 128)])
        for s in range(4):
            S.op("dve", lambda e, s=s: e.tensor_scalar(tm.t[:, s, :], tm.t[:, s, :], rs.t[:, s, 0:1], rs.t[:, s, 1:2], ALU.mult, ALU.add),
                 reads=[tm.reg(s), rs.reg(s)], writes=[tm.reg(s)])
            ln_affine(s, 0)
        if KSTOP < 6:
            return
        for j in range(NJ):
            wb = load_w(60 + j)
            G, U = nb(), nb()
            for h, b in ((0, G), (1, U)):
                pairs = [(wb.t[:, kc * 256 + h * 128: kc * 256 + h * 128 + 128], hT.t[:, kc, :]) for kc in range(16)]
                S.op("pe", lambda e, b=b, pairs=pairs: mm_group(e, b, T, pairs), reads=[wb.reg(), hT.reg()], writes=[P(b)])
            sg = nscr()
            S.op("act", lambda e, sg=sg, G=G: e.activation(sg.t[:, :], ps[G][:, :], AF.Silu), reads=[P(G)], writes=[sg.reg()])
            S.op("dve", lambda e, sg=sg, U=U, j=j: e.tensor_tensor(act.t[:, j, :], ps[U][:, :], sg.t[:, :], ALU.mult),
                 reads=[P(U), sg.reg()], writes=[act.reg(j)])
        if KSTOP < 7:
            return
        tokmajor_matmul(act, NJ, 104, 6)
        if ti + 1 < n_tiles:
            do_stage0(ti + 1)

    def ln2_hooks(ti):
        def mk(s):
            def h():
                ln_norm(s, s)
                ln_affine(s, 2)
                tok = S.dma("sp", out_d[ti * T + s * 128: ti * T + (s + 1) * 128, :], tm.t[:, s, :], "out%d" % s, reads=[tm.reg(s)])
                out_toks.append(tok)
            return h
        return [mk(s) for s in range(4)]

    stage0(xh_d[:, :], HALO, xTh, (0, HALO))
    stage1(xTh, HALO, True)
    do_stage0(0)
    hooks = ()
    for ti in range(n_tiles):
        tile_body(ti, hooks)
        hooks = ln2_hooks(ti)
    for h in hooks:
        h()
    for tok in out_toks[-4:]:
        S.wait_tok("sp", tok)
    if not out_toks:
        tok = S.dma("sp", out_d[0:128, :], gb.t[:, 0, :], "out0", reads=[gb.reg(0)])
        S.wait_tok("sp", tok)
    with nc.Block() as block:
        @block.tensor
        def _(e):
            for f in S.streams["pe"]:
                f(e)

        @block.scalar
        def _(e):
            for f in S.streams["act"]:
                f(e)

        @block.vector
        def _(e):
            for f in S.streams["dve"]:
                f(e)

        @block.gpsimd
        def _(e):
            for f in S.streams["pool"]:
                f(e)

        @block.sync
        def _(e):
            for f in S.streams["sp"]:
                f(e)
    return nc


def pack_weights(w_in, dw_kernel, w_conv_out, w_pool, w_out, w_ffn_in, w_ffn_out):
    wq = np.zeros((NSLOT, 128, SLOT), np.float32)

    def pair16(W, cols):
        sub = W[:, cols]
        return sub.reshape(16, 128, 256).transpose(1, 0, 2).reshape(128, 4096)

    ar = np.arange(128)
    i = 0
    for j in range(8):
        wq[i] = pair16(w_in, np.concatenate([j * 128 + ar, 1024 + j * 128 + ar])); i += 1
    for m in range(4):
        wq[i] = pair16(w_in, 2048 + m * 256 + np.arange(256)); i += 1
    for j in range(8):
        blk = np.zeros((128, KT, 128), np.float32)
        blk[ar, :, ar] = dw_kernel[:, j * 128:(j + 1) * 128].T
        wq[i, :, :KT * 128] = blk.reshape(128, KT * 128); i += 1
    for j in range(16):
        cvo = w_conv_out[:, j * 128:(j + 1) * 128].reshape(8, 128, 128).transpose(1, 0, 2).reshape(128, 1024)
        g = j // 4
        pw = w_pool[g][:, (j % 4) * 128:(j % 4 + 1) * 128].reshape(2, 128, 128).transpose(1, 0, 2).reshape(128, 256)
        wq[i, :, 0:1024] = cvo
        wq[i, :, 1024:1280] = pw
        i += 1
        wq[i] = pair16(w_in, np.concatenate([3072 + j * 128 + ar, 5120 + j * 128 + ar])); i += 1
    for n in range(4):
        for kg in range(2):
            blk = w_out[kg * 1024:(kg + 1) * 1024, n * 512:(n + 1) * 512].reshape(8, 128, 512).transpose(1, 0, 2)
            wq[i] = blk.reshape(128, 4096); i += 1
    for j in range(NJ):
        wq[i] = pair16(w_ffn_in, np.concatenate([j * 128 + ar, FH + j * 128 + ar])); i += 1
    for n in range(4):
        for kg in range(6):
            k0 = kg * 8
            k1 = min(NJ, k0 + 8)
            blk = w_ffn_out[k0 * 128:k1 * 128, n * 512:(n + 1) * 512].reshape(k1 - k0, 128, 512).transpose(1, 0, 2)
            wq[i, :, :(k1 - k0) * 512] = blk.reshape(128, (k1 - k0) * 512); i += 1
    assert i == NSLOT
    return wq


def pack_consts(dw_bias, conv_ln_g, conv_ln_b, pool_scale, ln1_g, ln1_b):
    cp = np.ones((128, NCP), np.float32)
    cp[:, CP_DWB:CP_DWB + 8] = dw_bias.reshape(8, 128).T
    cp[:, CP_LNG:CP_LNG + 8] = conv_ln_g.reshape(8, 128).T
    cp[:, CP_LNB:CP_LNB + 8] = conv_ln_b.reshape(8, 128).T
    cp[:, CP_PSC:CP_PSC + 16] = pool_scale.reshape(16, 128).T
    cp[:, CP_EPS] = EPS
    cp[:, CP_G1:CP_G1 + 16] = ln1_g.reshape(16, 128).T
    cp[:, CP_B1:CP_B1 + 16] = ln1_b.reshape(16, 128).T
    return cp


def start_corr():
    c = np.ones((128, 4 * 16), np.float32)
    for g in range(4):
        w = 2 ** (g + 1)
        for t in range(16):
            c[:, g * 16 + t] = w / min(t + 1, w)
    return c


_PROGRAMS = {}


def run_cores(x_rows_list, halo_list, start_flags, wq, cp, gbb, n_tiles):
    if n_tiles not in _PROGRAMS:
        _PROGRAMS[n_tiles] = build_program(n_tiles)
    nc = _PROGRAMS[n_tiles]
    mats = np.stack([np.eye(128, dtype=np.float32), np.full((128, 128), 1.0 / CW, np.float32)])
    corr = start_corr()
    in_maps = []
    for xr, xh, sf in zip(x_rows_list, halo_list, start_flags):
        cpc = cp.copy()
        if sf:
            cpc[:, CP_CORR:CP_CORR + 64] = corr
        in_maps.append({"x": np.ascontiguousarray(xr), "xh": np.ascontiguousarray(xh), "wq": wq, "cp": cpc, "gb": gbb, "mats": mats})
    res = run_bass_kernel_spmd(nc, in_maps, core_ids=list(range(len(in_maps))))
    return [r["out"] for r in res.results]


def kernel(x, w_in, dw_kernel, dw_bias, conv_ln_g, conv_ln_b, w_conv_out, w_pool, pool_scale, w_out,
           ln1_g, ln1_b, w_ffn_in, w_ffn_out, ln2_g, ln2_b):
    x = np.asarray(x, np.float32)
    B, SEQ, _ = x.shape
    f = lambda a: np.asarray(a, np.float32)[0]
    wq = pack_weights(f(w_in), f(dw_kernel), f(w_conv_out), f(w_pool), f(w_out), f(w_ffn_in), f(w_ffn_out))
    cp = pack_consts(f(dw_bias), f(conv_ln_g), f(conv_ln_b), f(pool_scale), f(ln1_g), f(ln1_b))
    gbb = np.stack([np.broadcast_to(f(a)[None, :], (128, D)) for a in (ln1_g, ln1_b, ln2_g, ln2_b)]).astype(np.float32)
    per_core = (B * SEQ) // NCORES
    n_tiles = per_core // T
    xs, hs, sf = [], [], []
    xf = x.reshape(B * SEQ, D)
    for c in range(NCORES):
        r0 = c * per_core
        xs.append(xf[r0:r0 + per_core])
        if r0 % SEQ == 0:
            hs.append(np.zeros((HALO, D), np.float32)); sf.append(True)
        else:
            hs.append(xf[r0 - HALO:r0]); sf.append(False)
    outs = run_cores(xs, hs, sf, wq, cp, gbb, n_tiles)
    return np.concatenate(outs, axis=0).reshape(B, SEQ, D).astype(np.float32)
```
